# Optimizing a Trainium2 kernel written in Bass

```python
import math
import jax, jax.numpy as jnp
from jax import lax
import numpy as np

D_MODEL = 1024
BATCH = 8
SEQ = 2048
DEPTH = 1
DEC_BATCH = 128
DEC_SEQ = 1
PAST_LEN = 16384
PAGE_SIZE = 128

D_MIX = 2 * D_MODEL
SSD_WIDTH = (3 * D_MIX) // 4
SSD_HEAD_DIM = 64
SSD_HEADS = SSD_WIDTH // SSD_HEAD_DIM
SSD_GROUPS = 4
SSD_HEADS_PER_GROUP = SSD_HEADS // SSD_GROUPS
SSD_STATE = 64
SSD_CONV_W = 4
SSD_CHUNK = 128
CONV_DIM = SSD_WIDTH + 2 * SSD_GROUPS * SSD_STATE
POOL_WIDTH = D_MIX - SSD_WIDTH
POOL_WINDOWS = (2, 4, 8, 16)
POOL_GROUPS = len(POOL_WINDOWS)
POOL_GROUP_DIM = POOL_WIDTH // POOL_GROUPS
POOL_MAX = max(POOL_WINDOWS)
D_FF = 128 * ((8 * D_MODEL // 3 + 127) // 128)
FFN_CONV_W = 3
IN_COLS = SSD_WIDTH + CONV_DIM + SSD_HEADS + POOL_WIDTH
EPS = 1e-6

kernel_name = "hymba_ssd_pool_convffn_step"


def rmsnorm(x, w):
    xf = x.astype(jnp.float32)
    y = xf * lax.rsqrt(jnp.mean(xf * xf, axis=-1, keepdims=True) + EPS) * w.astype(jnp.float32)
    return y.astype(x.dtype)


def causal_dwconv(u, buf, w, b):
    K = w.shape[0]
    T = u.shape[1]
    ext = jnp.concatenate([buf.astype(u.dtype), u], axis=1)
    out = b
    for k in range(K):
        out = out + ext[:, k:k + T] * w[k]
    return out, ext[:, ext.shape[1] - (K - 1):]


def pad_time(arr, pad):
    return jnp.pad(arr, [(0, 0), (0, pad)] + [(0, 0)] * (arr.ndim - 2))


def ssd_scan(x, dt, a, bm, cm, h0):
    f32 = jnp.float32
    bsz, T = x.shape[0], x.shape[1]
    G, R, P, N = SSD_GROUPS, SSD_HEADS_PER_GROUP, SSD_HEAD_DIM, SSD_STATE
    Q = min(SSD_CHUNK, T)
    pad = (-T) % Q
    if pad:
        x, dt, bm, cm = pad_time(x, pad), pad_time(dt, pad), pad_time(bm, pad), pad_time(cm, pad)
    nc = (T + pad) // Q
    xc = x.astype(f32).reshape(bsz, nc, Q, G, R, P)
    dtc = dt.astype(f32).reshape(bsz, nc, Q, G, R)
    bc = bm.astype(f32).reshape(bsz, nc, Q, G, N)
    cc = cm.astype(f32).reshape(bsz, nc, Q, G, N)
    acum = jnp.cumsum(dtc * a.reshape(G, R), axis=2)
    diff = acum[:, :, :, None] - acum[:, :, None]
    causal = jnp.tril(jnp.ones((Q, Q), dtype=bool))[:, :, None, None]
    decay = jnp.exp(jnp.where(causal, diff, -jnp.inf))
    cb = jnp.einsum('bclgn,bcsgn->bclsg', cc, bc)
    mix = cb[..., None] * decay * dtc[:, :, None]
    y_diag = jnp.einsum('bclsgr,bcsgrp->bclgrp', mix, xc)
    decay_to_end = jnp.exp(acum[:, :, -1:] - acum)
    chunk_states = jnp.einsum('bclgn,bclgr,bclgrp->bcgrpn', bc, decay_to_end * dtc, xc)
    chunk_decay = jnp.exp(acum[:, :, -1])

    def step(h, inp):
        cs, cd = inp
        return cd[..., None, None] * h + cs, h

    h_init = h0.astype(f32).reshape(bsz, G, R, P, N)
    h_final, h_in = lax.scan(step, h_init,
                             (jnp.moveaxis(chunk_states, 1, 0), jnp.moveaxis(chunk_decay, 1, 0)))
    h_in = jnp.moveaxis(h_in, 0, 1)
    y_off = jnp.einsum('bclgn,bcgrpn,bclgr->bclgrp', cc, h_in, jnp.exp(acum))
    y = (y_diag + y_off).reshape(bsz, nc * Q, SSD_HEADS, P)[:, :T]
    return y, h_final.reshape(bsz, SSD_HEADS, P, N)


def pool_mixer(v, buf, pos0, pool_w, pool_scale):
    f32 = jnp.float32
    bsz, T = v.shape[0], v.shape[1]
    P = POOL_MAX - 1
    ext = jnp.concatenate([buf.astype(v.dtype), v], axis=1)
    cs = jnp.cumsum(ext.astype(f32), axis=1)
    cs = jnp.concatenate([jnp.zeros((bsz, 1, POOL_WIDTH), f32), cs], axis=1)
    pos = pos0 + jnp.arange(T, dtype=jnp.int32)
    vf = v.astype(f32)
    outs = []
    for g, w in enumerate(POOL_WINDOWS):
        c0, c1 = g * POOL_GROUP_DIM, (g + 1) * POOL_GROUP_DIM
        win_sum = cs[:, P + 1:P + 1 + T, c0:c1] - cs[:, P + 1 - w:P + 1 - w + T, c0:c1]
        cnt = jnp.minimum(pos + 1, w).astype(f32)[None, :, None]
        outs.append(win_sum / cnt - vf[:, :, c0:c1])
    m = jnp.stack(outs, axis=2)
    y = jnp.einsum('btgc,gcd->btgd', m, pool_w.astype(f32)).reshape(bsz, T, POOL_WIDTH)
    y = y * pool_scale.astype(f32)
    return y.astype(v.dtype), ext[:, ext.shape[1] - P:]


def hybrid_layer(x, pos0, h0, conv_buf, pool_buf, ffn_buf,
                 norm1_w, w_in, conv_w, conv_b, dt_bias, a_log, d_skip, ssd_norm_w,
                 pool_w, pool_scale, w_out, norm2_w, w_up, ffn_conv_w, ffn_conv_b, w_down):
    f32 = jnp.float32
    bsz, T, _ = x.shape
    hn = rmsnorm(x, norm1_w)
    proj = hn @ w_in
    z, xbc, dt_raw, v = jnp.split(
        proj, [SSD_WIDTH, SSD_WIDTH + CONV_DIM, SSD_WIDTH + CONV_DIM + SSD_HEADS], axis=-1)
    xbc, conv_new = causal_dwconv(xbc, conv_buf, conv_w, conv_b)
    xbc = jax.nn.silu(xbc)
    xs, bm, cm = jnp.split(xbc, [SSD_WIDTH, SSD_WIDTH + SSD_GROUPS * SSD_STATE], axis=-1)
    xs = xs.reshape(bsz, T, SSD_HEADS, SSD_HEAD_DIM)
    bm = bm.reshape(bsz, T, SSD_GROUPS, SSD_STATE)
    cm = cm.reshape(bsz, T, SSD_GROUPS, SSD_STATE)
    dt = jax.nn.softplus(dt_raw.astype(f32) + dt_bias.astype(f32))
    a = -jnp.exp(a_log.astype(f32))
    y_ssd, h_new = ssd_scan(xs, dt, a, bm, cm, h0)
    y_ssd = y_ssd + d_skip.astype(f32)[:, None] * xs.astype(f32)
    y_ssd = y_ssd.reshape(bsz, T, SSD_WIDTH) * jax.nn.silu(z.astype(f32))
    y_ssd = rmsnorm(y_ssd.reshape(bsz, T, SSD_GROUPS, SSD_WIDTH // SSD_GROUPS),
                    ssd_norm_w.reshape(SSD_GROUPS, SSD_WIDTH // SSD_GROUPS))
    y_ssd = y_ssd.reshape(bsz, T, SSD_WIDTH).astype(x.dtype)
    y_pool, pool_new = pool_mixer(v, pool_buf, pos0, pool_w, pool_scale)
    x = x + jnp.concatenate([y_ssd, y_pool], axis=-1) @ w_out
    hn = rmsnorm(x, norm2_w)
    u = hn @ w_up
    u, ffn_new = causal_dwconv(u, ffn_buf, ffn_conv_w, ffn_conv_b)
    gate, val = jnp.split(u, 2, axis=-1)
    x = x + (jax.nn.silu(gate) * val) @ w_down
    return x, h_new.astype(x.dtype), conv_new, pool_new, ffn_new


def setup_inputs(seed: int = 0) -> dict:
    key = jax.random.key(seed)
    ks = jax.random.split(key, 24)
    f32 = jnp.float32
    L = DEPTH

    def nrm(k, shape, scale):
        return jax.random.normal(k, shape, f32) * scale

    dt0 = jnp.exp(jax.random.uniform(ks[10], (L, SSD_HEADS), f32, math.log(1e-3), math.log(1e-1)))
    return {
        "x_prompt": nrm(ks[0], (BATCH, SEQ, D_MODEL), 1.0),
        "x_sample": nrm(ks[1], (DEC_BATCH, DEC_SEQ, D_MODEL), 1.0),
        "state_ssm": nrm(ks[2], (L, DEC_BATCH, SSD_HEADS, SSD_HEAD_DIM, SSD_STATE), 0.1),
        "state_conv": nrm(ks[3], (L, DEC_BATCH, SSD_CONV_W - 1, CONV_DIM), 1.0),
        "state_pool": nrm(ks[4], (L, DEC_BATCH, POOL_MAX - 1, POOL_WIDTH), 1.0),
        "state_ffn_conv": nrm(ks[5], (L, DEC_BATCH, FFN_CONV_W - 1, 2 * D_FF), 1.0),
        "norm1_w": 1.0 + nrm(ks[6], (L, D_MODEL), 0.02),
        "w_in": nrm(ks[7], (L, D_MODEL, IN_COLS), D_MODEL ** -0.5),
        "conv_w": nrm(ks[8], (L, SSD_CONV_W, CONV_DIM), SSD_CONV_W ** -0.5),
        "conv_b": nrm(ks[9], (L, CONV_DIM), 0.02),
        "dt_bias": dt0 + jnp.log(-jnp.expm1(-dt0)),
        "a_log": jnp.log(jax.random.uniform(ks[11], (L, SSD_HEADS), f32, 1.0, 16.0)),
        "d_skip": 1.0 + nrm(ks[12], (L, SSD_HEADS), 0.1),
        "ssd_norm_w": 1.0 + nrm(ks[13], (L, SSD_WIDTH), 0.02),
        "pool_w": nrm(ks[14], (L, POOL_GROUPS, POOL_GROUP_DIM, POOL_GROUP_DIM), POOL_GROUP_DIM ** -0.5),
        "pool_scale": 1.0 + nrm(ks[15], (L, POOL_WIDTH), 0.1),
        "w_out": nrm(ks[16], (L, D_MIX, D_MODEL), D_MIX ** -0.5),
        "norm2_w": 1.0 + nrm(ks[17], (L, D_MODEL), 0.02),
        "w_up": nrm(ks[18], (L, D_MODEL, 2 * D_FF), D_MODEL ** -0.5),
        "ffn_conv_w": nrm(ks[19], (L, FFN_CONV_W, 2 * D_FF), FFN_CONV_W ** -0.5),
        "ffn_conv_b": nrm(ks[20], (L, 2 * D_FF), 0.02),
        "w_down": nrm(ks[21], (L, D_FF, D_MODEL), D_FF ** -0.5),
        "final_norm_w": 1.0 + nrm(ks[22], (D_MODEL,), 0.02),
    }


def reference(x_prompt, x_sample, state_ssm, state_conv, state_pool, state_ffn_conv,
              norm1_w, w_in, conv_w, conv_b, dt_bias, a_log, d_skip, ssd_norm_w,
              pool_w, pool_scale, w_out, norm2_w, w_up, ffn_conv_w, ffn_conv_b, w_down,
              final_norm_w):
    dtp = x_prompt.dtype
    xp, xs = x_prompt, x_sample
    ssm_p, ssm_s, conv_p, conv_s, pool_p, pool_s, ffn_p, ffn_s = [], [], [], [], [], [], [], []
    for l in range(DEPTH):
        params = (norm1_w[l], w_in[l], conv_w[l], conv_b[l], dt_bias[l], a_log[l], d_skip[l],
                  ssd_norm_w[l], pool_w[l], pool_scale[l], w_out[l], norm2_w[l], w_up[l],
                  ffn_conv_w[l], ffn_conv_b[l], w_down[l])
        xp, hp, cp, pp, fp = hybrid_layer(
            xp, 0,
            jnp.zeros((BATCH, SSD_HEADS, SSD_HEAD_DIM, SSD_STATE), dtp),
            jnp.zeros((BATCH, SSD_CONV_W - 1, CONV_DIM), dtp),
            jnp.zeros((BATCH, POOL_MAX - 1, POOL_WIDTH), dtp),
            jnp.zeros((BATCH, FFN_CONV_W - 1, 2 * D_FF), dtp),
            *params)
        xs, hs, cs_, ps, fs = hybrid_layer(
            xs, PAST_LEN, state_ssm[l], state_conv[l], state_pool[l], state_ffn_conv[l], *params)
        ssm_p.append(hp); ssm_s.append(hs)
        conv_p.append(cp); conv_s.append(cs_)
        pool_p.append(pp); pool_s.append(ps)
        ffn_p.append(fp); ffn_s.append(fs)
    y_prompt = rmsnorm(xp, final_norm_w)
    y_sample = rmsnorm(xs, final_norm_w)
    return (y_prompt, y_sample,
            jnp.stack(ssm_p), jnp.stack(ssm_s),
            jnp.stack(conv_p), jnp.stack(conv_s),
            jnp.stack(pool_p), jnp.stack(pool_s),
            jnp.stack(ffn_p), jnp.stack(ffn_s))
```

```python
import contextlib
import numpy as np
import concourse.bass as bass
import concourse.mybir as mybir
from concourse.bass_utils import run_bass_kernel_spmd

F32 = mybir.dt.float32
BF16 = mybir.dt.bfloat16
AF = mybir.ActivationFunctionType
ALU = mybir.AluOpType
AX = mybir.AxisListType

NCORES = 8
D = 1024
SEQ = 2048
NS = 16
H = 24
P = 64
NST = 64
G = 4
SSDW = 1536
CONVD = 2048
POOLW = 512
DFF = 2816
INC = 4120
EPS = 1e-6
T1 = 256
T2 = 512
NEGBIG = -30000.0

ENGS = ["pe", "act", "dve", "pool", "sp"]


class Prog:
    def __init__(self, nc, n_dma_sems=10):
        self.nc = nc
        self.stack = contextlib.ExitStack()
        self.ops = {e: [] for e in ENGS}
        self.cnt = {e: 0 for e in ENGS}
        self.sem = {}
        for e in ENGS[:4]:
            self.sem[e] = self.stack.enter_context(nc.semaphore("s_" + e))
        self.dma_sems = {}
        self.dma_sem_val = {}
        self.dma_rr = {}
        for q, n in (("sp", 14), ("act", 6), ("pool", 44)):
            self.dma_sems[q] = [self.stack.enter_context(nc.semaphore(f"d_{q}_{i}"))
                                for i in range(n)]
            self.dma_rr[q] = 0
        self.waited = {}
        self.lastw = {}
        self.readers = {}
        self.store_tokens = []

    def _need(self, eng, tok, waits):
        sem, val, teng = tok
        if teng == eng and eng == "pe":
            return
        k = (eng, id(sem))
        if self.waited.get(k, 0) >= val:
            return
        self.waited[k] = val
        waits.append((sem, val))

    def _deps(self, eng, reads, writes):
        waits = []
        for k in reads:
            t = self.lastw.get(k)
            if t is not None:
                self._need(eng, t, waits)
            if k.startswith("ps"):
                for t in self.readers.get(k, ()):
                    if t[2] != eng:
                        self._need(eng, t, waits)
        for k in writes:
            t = self.lastw.get(k)
            if t is not None:
                self._need(eng, t, waits)
            for t in self.readers.get(k, ()):
                self._need(eng, t, waits)
        return waits

    def _commit(self, tok, reads, writes):
        for k in reads:
            self.readers.setdefault(k, []).append(tok)
        for k in writes:
            self.lastw[k] = tok
            self.readers[k] = []

    def op(self, eng, fn, reads=(), writes=()):
        waits = self._deps(eng, reads, writes)
        self.cnt[eng] += 1
        tok = (self.sem[eng], self.cnt[eng], eng)
        self.ops[eng].append((waits, fn, (self.sem[eng], 1)))
        self._commit(tok, reads, writes)
        return tok

    def group(self, eng, fns, reads=(), writes=()):
        waits = self._deps(eng, reads, writes)
        self.cnt[eng] += 1
        tok = (self.sem[eng], self.cnt[eng], eng)
        n = len(fns)
        for i, fn in enumerate(fns):
            self.ops[eng].append((waits if i == 0 else [], fn,
                                  (self.sem[eng], 1) if i == n - 1 else None))
        self._commit(tok, reads, writes)
        return tok

    def dma(self, q, out, in_, reads=(), writes=(), store=False, **kw):
        waits = self._deps(q, reads, writes)
        i = self.dma_rr[q]
        self.dma_rr[q] = (i + 1) % len(self.dma_sems[q])
        sem = self.dma_sems[q][i]
        prev = self.dma_sem_val.get(id(sem), 0)
        if prev > 0:
            self._need(q, (sem, prev, None), waits)
        val = prev + 16
        self.dma_sem_val[id(sem)] = val
        tok = (sem, val, None)

        def fn(e, out=out, in_=in_, kw=kw):
            return e.dma_start(out=out, in_=in_, **kw)
        self.ops[q].append((waits, fn, (sem, 16)))
        self._commit(tok, reads, writes)
        if store:
            self.store_tokens.append(tok)
        return tok

    def flush(self, final=False):
        if final:
            waits = []
            for t in self.store_tokens:
                self._need("sp", t, waits)
            for e in ENGS[:4]:
                if self.cnt[e] > 0:
                    self._need("sp", (self.sem[e], self.cnt[e], e), waits)
            self.ops["sp"].append((waits, None, None))
        else:
            for e in ENGS:
                waits = []
                for e2 in ENGS[:4]:
                    if e2 != e and self.cnt[e2] > 0:
                        self._need(e, (self.sem[e2], self.cnt[e2], e2), waits)
                for q in self.dma_sems:
                    for sem in self.dma_sems[q]:
                        v = self.dma_sem_val.get(id(sem), 0)
                        if v > 0:
                            self._need(e, (sem, v, None), waits)
                self.ops[e].append((waits, None, None))
        ops = self.ops
        self.ops = {e: [] for e in ENGS}

        def replay(e, lst):
            for waits, fn, inc in lst:
                for sem, val in waits:
                    e.wait_ge(sem, val)
                if fn is None:
                    continue
                ins = fn(e)
                if inc is not None:
                    ins.then_inc(inc[0], inc[1])

        with self.nc.Block() as block:
            @block.tensor
            def _(e):
                replay(e, ops["pe"])

            @block.scalar
            def _(e):
                replay(e, ops["act"])

            @block.vector
            def _(e):
                replay(e, ops["dve"])

            @block.gpsimd
            def _(e):
                replay(e, ops["pool"])

            @block.sync
            def _(e):
                replay(e, ops["sp"])


def MM(out, lhsT, rhs, start=True, stop=True):
    return lambda e: e.matmul(out, lhsT, rhs, start=start, stop=stop)


def TR(out, in_, ident):
    return lambda e: e.transpose(out, in_, ident)


def ACT(out, in_, func, **kw):
    return lambda e: e.activation(out, in_, func, **kw)


def TT(out, a, b, op):
    return lambda e: e.tensor_tensor(out, a, b, op)


def TS(out, a, s1, s2, op0, op1=None):
    if op1 is None:
        return lambda e: e.tensor_scalar(out, a, s1, s2, op0)
    return lambda e: e.tensor_scalar(out, a, s1, s2, op0, op1)


def STT(out, in0, scalar, in1, op0, op1):
    return lambda e: e.scalar_tensor_tensor(out, in0, scalar, in1, op0, op1)


def CP(out, in_):
    return lambda e: e.tensor_copy(out, in_)


def MS(out, val):
    return lambda e: e.memset(out, val)


def pk(b, q0=0, q1=4):
    return [f"ps{b}"]


def bc(ap, shape):
    return ap.to_broadcast(list(shape))


def build_nc():
    nc = bass.Bass("TRN2", target_bir_lowering=False)

    def dram(name, shape, kind, dt=F32):
        return nc.dram_tensor(name, list(shape), dt, kind=kind).ap()

    EI, EO = "ExternalInput", "ExternalOutput"
    x_p = dram("x_p", [SEQ, D], EI)
    x_s = dram("x_s", [NS, D], EI)
    st_ssm = dram("st_ssm", [NS, H, P * NST], EI)
    st_conv = dram("st_conv", [NS, 3, CONVD], EI)
    st_pool = dram("st_pool", [NS, 15, POOLW], EI)
    st_ffn = dram("st_ffn", [NS, 2, 2 * DFF], EI)
    norm1_w = dram("norm1_w", [D // 128, 128], EI)
    w_in = dram("w_in", [D, INC], EI)
    conv_w = dram("conv_w", [4 * 16, 128], EI)
    conv_b = dram("conv_b", [16, 128], EI)
    dt_bias = dram("dt_bias", [1, H], EI)
    a_log = dram("a_log", [1, H], EI)
    d_skip = dram("d_skip", [1, H], EI)
    ssd_norm_w = dram("ssd_norm_w", [12, 128], EI)
    pool_w = dram("pool_w", [4, 128, 128], EI)
    pool_scale = dram("pool_scale", [4, 128], EI)
    w_out = dram("w_out", [2 * D, D], EI)
    norm2_w = dram("norm2_w", [D // 128, 128], EI)
    w_up = dram("w_up", [D, 2 * DFF], EI)
    ffn_conv_w = dram("ffn_conv_w", [3 * 44, 128], EI)
    ffn_conv_b = dram("ffn_conv_b", [44, 128], EI)
    w_down = dram("w_down", [DFF, D], EI)
    final_norm_w = dram("final_norm_w", [1, D], EI)

    y_p = dram("y_p", [SEQ, D], EO)
    y_s = dram("y_s", [NS, D], EO)
    ssm_p = dram("ssm_p", [H, P, NST], EO)
    ssm_s = dram("ssm_s", [NS, H, P * NST], EO)
    conv_p = dram("conv_p", [3, CONVD], EO)
    conv_s = dram("conv_s", [NS, 3, CONVD], EO)
    pool_p = dram("pool_p", [15, POOLW], EO)
    pool_s = dram("pool_s", [NS, 15, POOLW], EO)
    ffn_p = dram("ffn_p", [2, 2 * DFF], EO)
    ffn_s = dram("ffn_s", [NS, 2, 2 * DFF], EO)

    x2s = dram("x2s", [SEQ + NS, D], "Internal")
    scr_dtx = dram("scr_dtx", [NS, SSDW], "Internal")
    scr_dec = dram("scr_dec", [NS, H], "Internal")
    scr_B = dram("scr_B", [NS, 8, 64], "Internal")
    scr_C = dram("scr_C", [NS, 8, 64], "Internal")
    scr_y = dram("scr_y", [NS, SSDW], "Internal")

    p = Prog(nc)
    gst = p.stack

    def sbg(name, shape, dt=F32):
        return gst.enter_context(nc.sbuf_tensor(name, list(shape), dt))

    ps = gst.enter_context(nc.psum_tensor("ps", [128, 8, 512], F32))

    ident = sbg("ident", [128, 128])
    identb = sbg("identb", [128, 128], BF16)
    tri = sbg("tri", [128, 128])
    ones = sbg("ones", [128, 128])
    negtri = sbg("negtri", [128, 128], BF16)
    tri16 = sbg("tri16", [128, 128], BF16)
    vecA = sbg("vecA", [128, 112])
    vecB = sbg("vecB", [128, 176])
    a_bc = sbg("a_bc", [128, H])
    dtb_bc = sbg("dtb_bc", [128, H])
    D_bc = sbg("D_bc", [128, H])
    fnw_bc = sbg("fnw_bc", [128, D])
    invc = sbg("invc", [128, 16])
    stage_v = sbg("stage_v", [128, 128])
    eps_t = sbg("eps_t", [128, 1])

    n1w = vecA[:, 0:8]
    n2w = vecA[:, 8:16]
    ssdw = vecA[:, 96:108]
    pscale = vecA[:, 108:112]

    def cw(k, c):
        return vecA[:, 16 + k * 16 + c:17 + k * 16 + c]

    def cb(c):
        return vecA[:, 80 + c:81 + c]

    def fw(k, c):
        return vecB[:, k * 44 + c:k * 44 + c + 1]

    def fb(c):
        return vecB[:, 132 + c:133 + c]

    p.op("pool", MS(ident[:], 1.0), writes=["ident"])
    p.op("pool", lambda e: e.affine_select(ident[:], ident[:], [[-1, 128]], ALU.is_equal, 0.0,
                                            base=0, channel_multiplier=1), reads=["ident"], writes=["ident"])
    p.op("pool", MS(tri[:], 1.0), writes=["tri"])
    p.op("pool", lambda e: e.affine_select(tri[:], tri[:], [[1, 128]], ALU.is_ge, 0.0,
                                            base=0, channel_multiplier=-1), reads=["tri"], writes=["tri"])
    p.op("pool", MS(ones[:], 1.0), writes=["ones"])
    p.op("pool", MS(negtri[:], NEGBIG), writes=["negtri"])
    p.op("pool", lambda e: e.affine_select(negtri[:], negtri[:], [[-1, 128]], ALU.is_gt, 0.0,
                                            base=0, channel_multiplier=1), reads=["negtri"], writes=["negtri"])
    p.op("dve", CP(identb[:], ident[:]), reads=["ident"], writes=["identb"])
    p.op("dve", CP(tri16[:], tri[:]), reads=["tri"], writes=["tri16"])
    p.op("dve", MS(eps_t[:], EPS), writes=["eps"])
    p.op("pool", lambda e: e.iota(invc[:], [[1, 16]], base=1, channel_multiplier=0,
                                  allow_small_or_imprecise_dtypes=True), writes=["invc"])
    p.op("dve", lambda e: e.reciprocal(invc[:], invc[:]), reads=["invc"], writes=["invc"])

    def load_vec_T(dst, pieces, tag):
        r0 = 0
        for src, nrows in pieces:
            p.dma("sp", stage_v[r0:r0 + nrows, :], src, writes=["stage_v"])
            r0 += nrows
        p.op("pe", TR(ps[:, 7, 0:r0], stage_v[0:r0, :], ident[0:r0, 0:r0]),
             reads=["stage_v", "ident"], writes=pk(7))
        p.op("dve", CP(dst, ps[:, 7, 0:r0]), reads=pk(7), writes=[tag])

    load_vec_T(vecA[:, 0:112], [(norm1_w, 8), (norm2_w, 8), (conv_w, 64), (conv_b, 16),
                                (ssd_norm_w, 12), (pool_scale, 4)], "vecA")
    load_vec_T(vecB[:, 0:88], [(ffn_conv_w[0:88, :], 88)], "vecB")
    load_vec_T(vecB[:, 88:176], [(ffn_conv_w[88:132, :], 44), (ffn_conv_b, 44)], "vecB")

    p.dma("sp", a_bc[:], a_log.partition_broadcast(128), writes=["a_bc"])
    p.dma("sp", dtb_bc[:], dt_bias.partition_broadcast(128), writes=["dtb_bc"])
    p.dma("sp", D_bc[:], d_skip.partition_broadcast(128), writes=["D_bc"])
    p.dma("sp", fnw_bc[:], final_norm_w.partition_broadcast(128), writes=["fnw_bc"])
    p.op("act", ACT(a_bc[:], a_bc[:], AF.Exp), reads=["a_bc"], writes=["a_bc"])
    p.op("dve", TS(a_bc[:], a_bc[:], -1.0, None, ALU.mult), reads=["a_bc"], writes=["a_bc"])

    def rstd_from_ssq(ssq, n, nrows, tag):
        p.op("act", ACT(ssq, ssq, AF.Ln, scale=1.0 / n, bias=eps_t[0:nrows, :]), reads=[tag, "eps"], writes=[tag])
        p.op("act", ACT(ssq, ssq, AF.Exp, scale=-0.5), reads=[tag], writes=[tag])

    def rows_out(src_fn, C, R, dst, stage, tag, banks=(6, 7), skey="junk"):
        bi = 0
        for c0 in range(0, C, 4):
            cn = min(4, C - c0)
            b = banks[bi % len(banks)]
            bi += 1
            p.group("pe", [TR(ps[0:R, b, i * 128:(i + 1) * 128], src_fn(c0 + i), ident[:, :])
                           for i in range(cn)], reads=list(tag) + ["ident"], writes=pk(b))
            if isinstance(stage, list):
                stg, sk = stage[(bi - 1) % len(stage)], f"{skey}{(bi - 1) % len(stage)}"
            else:
                stg, sk = stage, skey
            p.op("act", ACT(stg[0:R, 0:cn * 128], ps[0:R, b, 0:cn * 128], AF.Copy),
                 reads=pk(b), writes=[sk])
            p.dma("sp", dst[:, c0 * 128:(c0 + cn) * 128], stg[0:R, 0:cn * 128],
                  reads=[sk], store=True)

    with contextlib.ExitStack() as st1:
        def sb1(name, shape, dt=F32):
            return st1.enter_context(nc.sbuf_tensor(name, list(shape), dt))

        WA = sb1("WA", [128, 8 * INC + 16 * D], BF16)
        w_in_sb = WA[:, 0:8 * INC].rearrange("p (k c) -> p k c", k=8)
        w_out_sb = WA[:, 8 * INC:8 * INC + 16 * D].rearrange("p (k c) -> p k c", k=16)
        pw16 = sb1("pw16", [128, 4, 128], BF16)

        for kc in range(8):
            p.dma("pool", w_in_sb[:, kc, SSDW:INC], w_in[kc * 128:(kc + 1) * 128, SSDW:INC],
                  writes=[f"w_in_b{kc}"])
        for g in range(4):
            p.dma("pool", pw16[:, g, :], pool_w[g], writes=["pw16"])
        for kc in range(8):
            p.dma("pool", w_in_sb[:, kc, 0:SSDW], w_in[kc * 128:(kc + 1) * 128, 0:SSDW],
                  writes=[f"w_in_a{kc}"])
        for kc in range(16):
            p.dma("pool", w_out_sb[:, kc, :], w_out[kc * 128:(kc + 1) * 128, :], writes=[f"w_out{kc}"])
        WINA = [f"w_in_a{kc}" for kc in range(8)]
        WINB = [f"w_in_b{kc}" for kc in range(8)]
        WOUT = [f"w_out{kc}" for kc in range(16)]

        junk = sb1("junk", [128, D], BF16)
        xin = [sb1(f"xin{i}", [128, D]) for i in range(2)]
        hnT = [sb1(f"hnT{i}", [128, 8, T1], BF16) for i in range(2)]
        ssq = sb1("ssq", [128, 8])
        ssq4 = sb1("ssq4", [128, 4])

        def norm_front(nrows, xin_t, xin_key, sc):
            sq = ssq[0:nrows, sc:sc + 1]
            p.op("act", ACT(junk[0:nrows, :], xin_t[0:nrows, :], AF.Square, accum_out=sq),
                 reads=[xin_key], writes=["junk", f"ssq{sc}"])
            rstd_from_ssq(sq, D, nrows, f"ssq{sc}")
            p.op("act", ACT(xin_t[0:nrows, :], xin_t[0:nrows, :], AF.Copy, scale=sq),
                 reads=[xin_key, f"ssq{sc}"], writes=[xin_key])

        def norm_back(nrows, xin_t, xin_key, dstT, col0, nw, wkeys, banks=(0, 1)):
            for half in range(2):
                b = banks[half]
                p.group("pe", [TR(ps[:, b, i * nrows:(i + 1) * nrows],
                                  xin_t[0:nrows, (half * 4 + i) * 128:(half * 4 + i + 1) * 128],
                                  ident[0:nrows, 0:nrows]) for i in range(4)],
                        reads=[xin_key, "ident"], writes=pk(b))
                p.op("dve", TT(dstT[:, half * 4:half * 4 + 4, col0:col0 + nrows],
                               ps[:, b, 0:4 * nrows].rearrange("p (k t) -> p k t", k=4),
                               bc(nw[:, half * 4:half * 4 + 4].unsqueeze(2), [128, 4, nrows]), ALU.mult),
                     reads=pk(b) + ["vecA"], writes=wkeys)

        def norm_transpose(src_rows, nrows, dstT, col0, nw, xin_t, xin_key, wkeys):
            norm_front(nrows, xin_t, xin_key, 2)
            norm_back(nrows, xin_t, xin_key, dstT, col0, nw, wkeys)

        def softplus_dt(dst, src_ps, nrows, rk, wk):
            p.op("dve", TT(dst, src_ps, dtb_bc[0:nrows, :], ALU.add), reads=rk + ["dtb_bc"], writes=[wk])
            p.op("act", ACT(dst, dst, AF.Exp), reads=[wk], writes=[wk])
            p.op("act", ACT(dst, dst, AF.Ln, bias=1.0, scale=1.0), reads=[wk], writes=[wk])

        def stageA_dma(ti):
            for j in range(2):
                p.dma("sp", xin[j][:], x_p[ti * T1 + j * 128: ti * T1 + (j + 1) * 128, :], writes=[f"xin{j}"])

        def stageA_norm(ti):
            for j in range(2):
                norm_front(128, xin[j], f"xin{j}", j)

        def stageA_tr(ti, banks):
            for j in range(2):
                norm_back(128, xin[j], f"xin{j}", hnT[ti % 2], j * 128, n1w, [f"hnT{ti % 2}_{j}"], banks=banks)

        def stage_BCD(ti, cs):
            hT = hnT[ti % 2]
            HN = [f"hnT{ti % 2}_0", f"hnT{ti % 2}_1"]
            for j in range(2):
                p.group("pe", [MM(ps[:, 2, j * 128:j * 128 + H], hT[:, kc, j * 128:(j + 1) * 128],
                                  w_in_sb[:, kc, 3584:3608], start=(kc == 0), stop=(kc == 7))
                               for kc in range(8)], reads=[HN[j]] + WINB, writes=pk(2))
            for j in range(2):
                softplus_dt(dt_tok[:, j, :], ps[:, 2, j * 128:j * 128 + H], 128, pk(2), f"dt_tok{j}")
                p.op("act", ACT(lndt[:, j, :], dt_tok[:, j, :], AF.Ln), reads=[f"dt_tok{j}"], writes=[f"lndt{j}"])
            stage_D_front(ti, hT, HN)
            Z_ops.clear()
            for j in range(2):
                for g3 in range(3):
                    def zg(j=j, g3=g3):
                        p.group("pe", [MM(ps[:, g3, :], hT[:, kc, j * 128:(j + 1) * 128], w_in_sb[:, kc, g3 * 512:(g3 + 1) * 512],
                                          start=(kc == 0), stop=(kc == 7)) for kc in range(8)],
                                reads=[HN[j]] + WINA, writes=pk(g3))
                    Z_ops.append(zg)

                def zev(j=j):
                    p.op("act", ACT(zsil[j][:], ps[:, 0:3, :].rearrange("p a b -> p (a b)"), AF.Silu),
                         reads=pk(0) + pk(1) + pk(2), writes=[f"zsil{j}"])
                Z_ops.append(zev)
            dback = stage_D_back_ops(ti)
            step = 0
            for _ in stage_C_gen(ti, hT, HN, cs, (3, 4, 7), (5, 6)):
                if Z_ops:
                    Z_ops.pop(0)()
                step += 1
                if step > 10 and not D_pool:
                    for _k in range(2):
                        if dback:
                            dback.pop(0)()
            while Z_ops:
                Z_ops.pop(0)()
            while D_pool:
                D_pool.pop(0)()
            while dback:
                dback.pop(0)()

        def stage_C_gen(ti, hnT, HN, cs, CB3, TB):
            pend_mid = []
            pend_silu = []
            pend_tr = []
            xtk = x_tok[ti % 2]
            xtkey = f"x_tok{ti % 2}"
            for idx, c in enumerate(cs):
                b = CB3[idx % len(CB3)]
                R = Rst[c % 2]
                rk = f"Rst{c % 2}"
                accp = ps[:, b, 0:T1]
                col = SSDW + c * 128
                p.group("pe", [MM(accp, w_in_sb[:, kc, col:col + 128], hnT[:, kc, :],
                                  start=(kc == 0), stop=(kc == 7)) for kc in range(8)],
                        reads=HN + WINB, writes=pk(b))
                p.op("pool", CP(R[:, 0:3], halo16[:, c, :]), reads=[f"halo{c}"], writes=[rk + "h"])
                p.op("act", ACT(R[:, 3:3 + T1], accp, AF.Copy), reads=pk(b), writes=[rk])
                if ti == NT1_ - 1:
                    p.op("act", ACT(halo[:, c, :], accp[:, T1 - 3:T1], AF.Copy), reads=pk(b), writes=[f"halo32_{c}"])

                def mid(c=c, b=b, accp=accp, R=R, rk=rk):
                    p.op("pe", lambda e: e.matmul(accp, cdiag[:, c, :], R[:, 2:2 + T1], start=False, stop=True,
                                                  skip_group_check=True),
                         reads=[rk, rk + "h", "cdiag"] + pk(b), writes=pk(b))
                    for k in range(2):
                        p.op("dve", STT(accp, R[:, k:k + T1], crat[:, k, c:c + 1], accp, ALU.mult, ALU.add),
                             reads=[rk, rk + "h", "crat"] + pk(b), writes=pk(b))
                    p.op("pool", CP(halo16[:, c, :], R[:, T1:T1 + 3]), reads=[rk], writes=[f"halo{c}"])
                pend_mid.append(mid)
                if len(pend_mid) > 1:
                    pend_mid.pop(0)()
                if D_pool:
                    D_pool.pop(0)()

                def do_silu(c=c, b=b, accp=accp):
                    if c < 14:
                        X = xsT[c % 3]
                        xk = f"xsT{c % 3}"
                        p.op("act", ACT(X[:], accp, AF.Silu, scale=w3p[:, c:c + 1], bias=cb(c)),
                             reads=pk(b) + ["crat", "vecA"], writes=[xk])
                        if c >= 12:
                            p.op("pool", CP(BT16[:, c - 12, :], X[:]), reads=[xk], writes=["BT16"])

                        def post(c=c, X=X, xk=xk):
                            tb = TB[c % len(TB)]
                            p.group("pe", [TR(ps[:, tb, jx * 128:(jx + 1) * 128], X[:, jx * 128:(jx + 1) * 128], ident[:])
                                           for jx in range(2)], reads=[xk, "ident"], writes=pk(tb))
                            src = ps[:, tb, 0:T1].rearrange("p (j t) -> p j t", j=2)
                            if c < 12:
                                if c % 2 == 0:
                                    p.op("act", ACT(xtk[:, :, c * 128:(c + 1) * 128], src, AF.Copy),
                                         reads=pk(tb), writes=[xtkey])
                                else:
                                    p.op("dve", CP(xtk[:, :, c * 128:(c + 1) * 128], src),
                                         reads=pk(tb), writes=[xtkey])
                            else:
                                for a in range(2):
                                    g = 2 * (c - 12) + a
                                    p.op("dve", CP(Bz[:, :, g, a * 64:(a + 1) * 64], src[:, :, a * 64:(a + 1) * 64]),
                                         reads=pk(tb), writes=["Bz"])
                        pend_tr.append(post)
                    else:
                        for a in range(2):
                            g = 2 * (c - 14) + a
                            p.op("act", ACT(Cz[a * 64:(a + 1) * 64, g, :], accp[a * 64:(a + 1) * 64, :], AF.Silu,
                                            scale=w3p[a * 64:(a + 1) * 64, c:c + 1], bias=cb(c)[a * 64:(a + 1) * 64, :]),
                                 reads=pk(b) + ["crat", "vecA"], writes=["Cz"])
                pend_silu.append(do_silu)
                if len(pend_silu) > 2:
                    pend_silu.pop(0)()
                if len(pend_tr) > 1:
                    pend_tr.pop(0)()
                yield
            while pend_mid:
                pend_mid.pop(0)()
            while pend_silu:
                pend_silu.pop(0)()
            while pend_tr:
                pend_tr.pop(0)()
            yield

        def stage_D_front(ti, hnT, HN):
            for c in range(4):
                b = 5 + (c % 2)
                col = 3608 + c * 128
                p.group("pe", [MM(ps[:, b, 0:T1], w_in_sb[:, kc, col:col + 128], hnT[:, kc, :],
                                  start=(kc == 0), stop=(kc == 7)) for kc in range(8)],
                        reads=HN + WINB, writes=pk(b))
                if ti > 0:
                    p.op("pool", CP(V[:, c, 0:15], V[:, c, T1:T1 + 15]), reads=[f"V{c}"], writes=[f"V{c}"])
                p.op("act", ACT(V[:, c, 15:15 + T1], ps[:, b, 0:T1], AF.Copy), reads=pk(b), writes=[f"V{c}"])
            L = 15 + T1
            D_fin.clear()
            D_pool.clear()
            for g in range(4):
                w = 2 << g
                cur = V[:, g, :]
                ck = f"V{g}"
                nsteps = g + 1
                step = 1
                lo = 0
                for si in range(nsteps):
                    if si == nsteps - 1:
                        nb, nk = SRES[g], f"SRES{g}"
                    else:
                        nb, nk = SSCR[si % 2], f"SSCR{si % 2}"
                    lo2 = lo + step

                    def sum_op(nb=nb, nk=nk, cur=cur, ck=ck, lo2=lo2, step=step):
                        p.op("pool", TT(nb[:, lo2:L], cur[:, lo2:L], cur[:, lo2 - step:L - step], ALU.add),
                             reads=[ck], writes=[nk])
                    D_pool.append(sum_op)
                    cur, ck, lo = nb, nk, lo2
                    step *= 2
                D_fin.append((g, w, cur, ck))

        def stage_D_back_ops(ti):
            L = 15 + T1
            ops = []
            for (g, w, cur, ck) in D_fin:
                def stt(g=g, w=w, cur=cur, ck=ck):
                    p.op("dve", STT(mT16[:, g, :], cur[:, 15:L], 1.0 / w, V[:, g, 15:L], ALU.mult, ALU.subtract),
                         reads=[ck, f"V{g}"], writes=[f"mT16_{g}"])
                    if ti == 0:
                        n = w - 1
                        p.op("dve", TT(ybuf[:, 0:n], cur[:, 15:15 + n], invc[:, 0:n], ALU.mult),
                             reads=[ck, "invc"], writes=["ybuf"])
                        p.op("dve", TT(mT16[:, g, 0:n], ybuf[:, 0:n], V[:, g, 15:15 + n], ALU.subtract),
                             reads=["ybuf", f"V{g}"], writes=[f"mT16_{g}"])
                ops.append(stt)
            for g in range(4):
                def mm(g=g):
                    b = g // 2
                    p.op("pe", MM(ps[:, b, (g % 2) * T1:(g % 2 + 1) * T1], pw16[:, g, :], mT16[:, g, :]),
                         reads=["pw16", f"mT16_{g}"], writes=pk(b))
                ops.append(mm)
            for g in range(4):
                def ev(g=g):
                    b = g // 2
                    p.op("act", ACT(ycatT[:, 12 + g, :], ps[:, b, (g % 2) * T1:(g % 2 + 1) * T1], AF.Copy,
                                    scale=pscale[:, g:g + 1]), reads=pk(b) + ["vecA"], writes=["ycatT_p"])
                ops.append(ev)
            return ops

        def chunk_prologue(ti, j):
            tsl = slice(j * 128, (j + 1) * 128)
            dtA, acum, nacum, tmp, eacum, cdec, w2 = (sm[:, i, :] for i in range(7))
            dtj = dt_tok[:, j, :]
            p.op("dve", TT(dtA, dtj, a_bc[:], ALU.mult), reads=[f"dt_tok{j}", "a_bc"], writes=["dtA"])
            p.op("dve", CP(dtA16[:, 0, :], dtA), reads=["dtA"], writes=["dtA16"])
            p.op("dve", TT(dtA16[:, 1, :], dtA, dtA16[:, 0, :], ALU.subtract), reads=["dtA", "dtA16"], writes=["dtA16"])
            p.group("pe", [MM(ps[:, 6, 0:H], tri[:], dtA), MM(ps[:, 6, 32:32 + H], ones[:], dtA)],
                    reads=["tri", "ones", "dtA"], writes=pk(6))
            p.group("pe", [MM(ps[:, 7, g * 128:(g + 1) * 128], BT16[:, g // 2, tsl], Cz[:, g, tsl])
                           for g in range(4)], reads=["BT16", "Cz"], writes=pk(7))
            p.op("dve", CP(acum, ps[:, 6, 0:H]), reads=pk(6), writes=["acum"])
            p.op("dve", TT(tmp, ps[:, 6, 32:32 + H], acum, ALU.subtract), reads=pk(6) + ["acum"], writes=["dte"])
            p.op("act", ACT(cdec, ps[:, 6, 32:32 + H], AF.Exp), reads=pk(6), writes=["cdec"])
            p.op("dve", TT(nacum, lndt[:, j, :], acum, ALU.subtract), reads=["acum", f"lndt{j}"], writes=["nacum"])
            p.op("act", ACT(CBsb[:].rearrange("p g l -> p (g l)"), ps[:, 7, :], AF.Copy), reads=pk(7), writes=["CBsb"])
            p.op("act", ACT(tmp, tmp, AF.Exp), reads=["dte"], writes=["dte"])
            p.op("act", ACT(eacum, acum, AF.Exp), reads=["acum"], writes=["eacum"])
            p.op("dve", TT(w2, tmp, dtj, ALU.mult), reads=["dte", f"dt_tok{j}"], writes=["w2"])
            for a in range(2):
                p.op("dve", CP(cdecH[a * 64:(a + 1) * 64, :].rearrange("p (jj r) -> p jj r", jj=2),
                               cdec[a * 64:(a + 1) * 64, :].rearrange("p (jj x) -> p jj x", jj=2)[:, :, 6 * a:6 * a + 6]),
                     reads=["cdec"], writes=["cdecH"])

        def ssd_chunk(ti, j, pre_heads=None, gap=None, defer_out=False, hidden=None, skip_prologue=False, early_next=None):
            hT = hnT[ti % 2]
            xtk = x_tok[ti % 2]
            xtkey = f"x_tok{ti % 2}"
            tok0 = ti * T1 + j * 128
            first = (ti == 0 and j == 0)
            last = (ti == SEQ // T1 - 1 and j == 1)
            tsl = slice(j * 128, (j + 1) * 128)
            dtA, acum, nacum, tmp, eacum, cdec, w2 = (sm[:, i, :] for i in range(7))
            dtj = dt_tok[:, j, :]
            if not skip_prologue:
                chunk_prologue(ti, j)
            xt3 = xtk[:, j, :].rearrange("p (h d) -> p h d", h=H)

            def b3(v):
                return bc(v.unsqueeze(2), [128, H, P])
            mms = []
            for g in range(4):
                c0 = g * 384
                rbase = (g // 2) * 384
                off = 0
                while off < 384:
                    col = c0 + off
                    n = min(384 - off, 512 - (col % 512))
                    mms.append(MM(ps[:, 3 + col // 512, (col % 512):(col % 512) + n], Cz[:, g, tsl],
                                  H16[:, rbase + off:rbase + off + n]))
                    off += n
            p.group("pe", mms, reads=["Cz", "H16"], writes=pk(3) + pk(4) + pk(5))
            p.op("dve", TT(y1b[:].rearrange("p (h d) -> p h d", h=H),
                           ps[:, 3:6, :].rearrange("p a (h d) -> p (a h) d", d=P), b3(eacum), ALU.mult),
                 reads=pk(3) + pk(4) + pk(5) + ["eacum"], writes=["y1b"])
            p.op("pool", TT(xw[:].rearrange("p (h d) -> p h d", h=H), xt3, b3(w2), ALU.mult),
                 reads=[xtkey, "w2"], writes=["xw"])
            p.op("pool", TT(H32[:].rearrange("p (h d) -> p h d", d=P), H32[:].rearrange("p (h d) -> p h d", d=P),
                            bc(cdecH[:].unsqueeze(2), [128, 12, P]), ALU.mult),
                 reads=["H32", "cdecH"], writes=["H32"])
            NB = 8

            def batch_front(k):
                ab = 6 + (k % 2)
                hs = range(4 * k, 4 * k + 4)
                mms = []
                for h in hs:
                    A_ps = ps[:, ab, (h % 4) * 128:(h % 4 + 1) * 128]
                    mms.append(MM(A_ps, bc(dtA16[:, 0, h:h + 1], [128, 128]), tri16[:], start=True, stop=False))
                    mms.append(MM(A_ps, bc(dtA16[:, 1, h:h + 1], [128, 128]), tri16[:], start=False, stop=False))
                    mms.append(MM(A_ps, identb[:], negtri[:], start=False, stop=True))
                p.group("pe", mms, reads=["dtA16", "tri16", "identb", "negtri"], writes=pk(ab))
                for h in hs:
                    A_ps = ps[:, ab, (h % 4) * 128:(h % 4 + 1) * 128]
                    p.op("act", ACT(Eh[h % NB][:], A_ps, AF.Exp, bias=nacum[:, h:h + 1], scale=1.0),
                         reads=pk(ab) + ["nacum"], writes=[f"Eh{h % NB}"])
                for h in hs:
                    p.op("dve", TT(Mh[h % NB][:], Eh[h % NB][:], CBsb[:, h // 6, :], ALU.mult),
                         reads=[f"Eh{h % NB}", "CBsb"], writes=[f"Mh{h % NB}"])

            def batch_back(k):
                for h in range(4 * k, 4 * k + 4):
                    col = h * 64
                    yo = ps[:, col // 512, (col % 512):(col % 512) + 64]
                    lastb = (h % 8 == 7)
                    p.group("pe", [lambda e, yo=yo, h=h, col=col: e.matmul(yo, Mh[h % NB][:], xtk[:, j, col:col + 64],
                                                                              start=False, stop=False, skip_group_check=True),
                                   lambda e, yo=yo, h=h, col=col, lastb=lastb: e.matmul(yo, Dmat[:, h, :], xtk[:, j, col:col + 64],
                                                                                         start=False, stop=lastb, skip_group_check=True)],
                            reads=[f"Mh{h % NB}", xtkey, "Dmat"], writes=pk(col // 512))
            batch_front(0)
            batch_front(1)
            p.group("pe", [lambda e, b=b: e.matmul(ps[:, b, :], identb[:], y1b[:, b * 512:(b + 1) * 512], start=True, stop=False,
                                                    skip_group_check=True) for b in range(3)],
                    reads=["identb", "y1b"], writes=pk(0) + pk(1) + pk(2))
            def hid(n):
                if hidden is not None:
                    for _ in range(n):
                        next(hidden, None)
            def state_update():
                for jj in range(2):
                    b = 3 + jj
                    p.group("pe", [MM(ps[:, b, 0:384], Bz[:, j, 2 * jj + a, :], xw[:, (2 * jj + a) * 384:(2 * jj + a + 1) * 384],
                                      start=(a == 0), stop=(a == 1)) for a in range(2)],
                            reads=["Bz", "xw"], writes=pk(b))
                for jj in range(2):
                    b = 3 + jj
                    p.op("dve", TT(H32[:, jj * 384:(jj + 1) * 384], ps[:, b, 0:384], H32[:, jj * 384:(jj + 1) * 384],
                                   ALU.add), reads=pk(b) + ["H32"], writes=["H32"])
                if not last:
                    p.op("act", ACT(H16[:], H32[:], AF.Copy), reads=["H32"], writes=["H16"])
            batch_back(0)
            state_update()
            hid(2)
            for k in range(2, 6):
                batch_front(k)
                batch_back(k - 1)
                hid(2)
            batch_back(5)
            hid(3)
            yd = ps[:, 0:3, :].rearrange("p a b -> p (a b)")
            p.op("dve", TT(ybuf[:], yd, zsil[j][:], ALU.mult), reads=pk(0) + pk(1) + pk(2) + [f"zsil{j}"], writes=["ybuf"])
            YK = ["ybuf"]
            for g in range(4):
                if g < 2:
                    p.op("act", ACT(junk[:, 0:384], ybuf[:, g * 384:(g + 1) * 384], AF.Square,
                                    accum_out=ssq4[:, g:g + 1]), reads=YK, writes=["junk", f"ssq4_{g}"])
                else:
                    p.op("dve", lambda e, g=g: e.scalar_tensor_tensor(junkB[:], ybuf[:, g * 384:(g + 1) * 384], 1.0,
                                                                     ybuf[:, g * 384:(g + 1) * 384], ALU.mult, ALU.mult,
                                                                     accum_out=ssq4[:, g:g + 1]),
                         reads=YK, writes=["junkB", f"ssq4_{g}"])
            SK = [f"ssq4_{g}" for g in range(4)]
            p.op("act", ACT(ssq4[:], ssq4[:], AF.Ln, scale=1.0 / 384, bias=eps_t[:]), reads=SK + ["eps"], writes=SK)
            p.op("act", ACT(ssq4[:], ssq4[:], AF.Exp, scale=-0.5), reads=SK, writes=SK)
            for g in range(4):
                if g % 2 == 0:
                    p.op("act", ACT(ybuf[:, g * 384:(g + 1) * 384], ybuf[:, g * 384:(g + 1) * 384], AF.Copy,
                                    scale=ssq4[:, g:g + 1]), reads=YK + SK, writes=[f"ybufN{g}"])
                else:
                    p.op("dve", TS(ybuf[:, g * 384:(g + 1) * 384], ybuf[:, g * 384:(g + 1) * 384], ssq4[:, g:g + 1], None,
                                   ALU.mult), reads=YK + SK, writes=[f"ybufN{g}"])
            if gap is not None:
                gap()
            YN = [f"ybufN{g}" for g in range(4)]
            for b3i in range(3):
                b = 3 + b3i
                p.group("pe", [TR(ps[:, b, i * 128:(i + 1) * 128], ybuf[:, (b3i * 4 + i) * 128:(b3i * 4 + i + 1) * 128],
                                  ident[:]) for i in range(4)], reads=YN + YK + ["ident"], writes=pk(b))
                if b3i != 1:
                    p.op("dve", TT(ycatT[:, b3i * 4:b3i * 4 + 4, tsl],
                                   ps[:, b, :].rearrange("p (k t) -> p k t", k=4),
                                   bc(ssdw[:, b3i * 4:b3i * 4 + 4].unsqueeze(2), [128, 4, 128]), ALU.mult),
                         reads=pk(b) + ["vecA"], writes=[f"ycatT_s{j}_{b3i}"])
                else:
                    p.group("act", [ACT(ycatT[:, 4 + i, tsl], ps[:, b, i * 128:(i + 1) * 128], AF.Copy,
                                        scale=ssdw[:, 4 + i:5 + i]) for i in range(4)],
                            reads=pk(b) + ["vecA"], writes=[f"ycatT_s{j}_{b3i}"])
            if early_next is not None:
                early_next()
            if pre_heads is not None:
                pre_heads()
            if not defer_out:
                do_out(ti, j, (0, 1))

        def do_out(ti, j, banks):
            tok0 = ti * T1 + j * 128
            xr = xres[j]
            p.dma("sp", xr[:], x_p[tok0:tok0 + 128, :], writes=[f"xres{j}"])
            out_proj(ycatT, slice(j * 128, (j + 1) * 128), 128, [f"ycatT_s{j}_{i}" for i in range(3)] + ["ycatT_p"],
                     xr, f"xres{j}", tok0, banks=banks)

        def out_proj(yT, tsl, nrows, ykeys, xr, xrk, row0, banks=(0, 1)):
            for hh in range(2):
                b = banks[hh]
                p.group("pe", [MM(ps[0:nrows, b, :], yT[:, kc, tsl], w_out_sb[:, kc, hh * 512:(hh + 1) * 512],
                                  start=(kc == 0), stop=(kc == 15)) for kc in range(16)],
                        reads=ykeys + WOUT, writes=pk(b))
                p.op("dve", TT(xr[0:nrows, hh * 512:(hh + 1) * 512], ps[0:nrows, b, :],
                               xr[0:nrows, hh * 512:(hh + 1) * 512], ALU.add),
                     reads=pk(b) + [xrk], writes=[xrk])
            p.dma("sp", x2s[row0:row0 + nrows, :], xr[0:nrows, :], reads=[xrk], writes=[f"x2s{row0}"])

        def sample_mixer():
            with contextlib.ExitStack() as sts:
                def sbs(name, shape, dt=F32):
                    return sts.enter_context(nc.sbuf_tensor(name, list(shape), dt))
                xs_in = sbs("xs_in", [NS, D])
                hnTs = sbs("hnTs", [128, 8, NS], BF16)
                dt_s = sbs("dt_s", [NS, H])
                zs = sbs("zs", [NS, SSDW])
                Rs = sbs("Rs", [128, 16, NS])
                stc_tok = sbs("stc_tok", [48, CONVD])
                stcT = sbs("stcT", [128, 16, NS, 3])
                accs = sbs("accs", [128, 16, NS])
                tmps = sbs("tmps", [128, 16, NS])
                xa_s = sbs("xa_s", [128, 16, NS])
                xbc_tok = sbs("xbc_tok", [NS, CONVD])
                vTs = sbs("vTs", [128, 4, NS])
                v_tok = sbs("v_tok", [NS, POOLW])
                stp_tok = [sbs(f"stp_tok{i}", [120, POOLW]) for i in range(2)]
                stpT = sbs("stpT", [128, 4, NS, 15])
                wsum = sbs("wsum", [128, NS])
                mTs = sbs("mTs", [128, 4, NS], BF16)
                ycatTs = sbs("ycatTs", [128, 16, NS], BF16)
                sms = sbs("sms", [NS, 4, H])
                dtx_s = sbs("dtx_s", [NS, SSDW])
                dtx_r = sbs("dtx_r", [128, 3, P])
                dec_r = sbs("dec_r", [128, 3])
                B_r = sbs("B_r", [128, NST])
                C_r = sbs("C_r", [128, NST])
                Hb = [sbs(f"Hb{i}", [128, 16, NST]) for i in range(3)]
                Tb2 = [sbs(f"Tb{i}", [128, 16, NST]) for i in range(2)]
                Tb = Tb2[0]
                Pb = sbs("Pb", [128, 16, NST])
                cb_r = sbs("cb_r", [128, 1])
                y_r = sbs("y_r", [128, 3, P])
                y_tok = sbs("y_tok", [NS, SSDW])

                p.dma("sp", xs_in[:], x_s, writes=["xs_in"])
                p.dma("sp", stc_tok[:], st_conv.rearrange("b k c -> (b k) c"), writes=["stc_tok"])
                stp_flat = st_pool.rearrange("b j c -> (b j) c")
                for i in range(2):
                    p.dma("sp", stp_tok[i][:], stp_flat[i * 120:(i + 1) * 120, :], writes=[f"stp_tok{i}"])
                norm_transpose(None, NS, hnTs, 0, n1w, xs_in, "xs_in", ["hnTs"])
                p.group("pe", [MM(ps[0:NS, 2, 0:H], hnTs[:, kc, :], w_in_sb[:, kc, 3584:3608],
                                  start=(kc == 0), stop=(kc == 7)) for kc in range(8)],
                        reads=["hnTs"] + WINB, writes=pk(2))
                softplus_dt(dt_s[:], ps[0:NS, 2, 0:H], NS, pk(2), "dt_s")
                for c in range(16):
                    col = SSDW + c * 128
                    p.group("pe", [MM(ps[:, 6, c * NS:(c + 1) * NS], w_in_sb[:, kc, col:col + 128], hnTs[:, kc, :],
                                      start=(kc == 0), stop=(kc == 7)) for kc in range(8)],
                            reads=["hnTs"] + WINB, writes=pk(6))
                p.op("act", ACT(Rs[:].rearrange("p c b -> p (c b)"), ps[:, 6, 0:16 * NS], AF.Copy),
                     reads=pk(6), writes=["Rs"])
                for half in range(2):
                    b = 0 + half
                    p.group("pe", [TR(ps[:, b, i * 48:(i + 1) * 48], stc_tok[0:48, (half * 8 + i) * 128:(half * 8 + i + 1) * 128],
                                      ident[0:48, 0:48]) for i in range(8)],
                            reads=["stc_tok", "ident"], writes=pk(b))
                    p.op("dve", CP(stcT[:, half * 8:half * 8 + 8, :, :].rearrange("p c b k -> p c (b k)"),
                                   ps[:, b, 0:384].rearrange("p (c x) -> p c x", c=8)),
                         reads=pk(b), writes=["stcT"])
                cw3 = vecA[:, 16:80].rearrange("p (k c) -> p k c", k=4)

                def bcw(k):
                    return bc(cw3[:, k, :].unsqueeze(2), [128, 16, NS])
                p.op("dve", TT(accs[:], Rs[:], bcw(3), ALU.mult), reads=["Rs", "vecA"], writes=["accs"])
                p.op("dve", TT(accs[:], accs[:], bc(vecA[:, 80:96].unsqueeze(2), [128, 16, NS]), ALU.add),
                     reads=["accs", "vecA"], writes=["accs"])
                for k in range(3):
                    p.op("dve", TT(tmps[:], stcT[:, :, :, k], bcw(k), ALU.mult), reads=["stcT", "vecA"], writes=["tmps"])
                    p.op("dve", TT(accs[:], accs[:], tmps[:], ALU.add), reads=["accs", "tmps"], writes=["accs"])
                p.op("act", ACT(xa_s[:], accs[:], AF.Silu), reads=["accs"], writes=["xa_s"])
                for pi, (src, sk) in enumerate(((Rs, "Rs"), (xa_s, "xa_s"))):
                    for q4 in range(4):
                        b = 0 + (q4 % 2)
                        p.group("pe", [TR(ps[0:NS, b, i * 128:(i + 1) * 128], src[:, q4 * 4 + i, :], ident[:])
                                       for i in range(4)], reads=[sk, "ident"], writes=pk(b))
                        p.op("act", ACT(xbc_tok[:, q4 * 512:(q4 + 1) * 512], ps[0:NS, b, :], AF.Copy),
                             reads=pk(b), writes=["xbc_tok"])
                    if pi == 0:
                        p.dma("sp", conv_s[:, 2, :], xbc_tok[:], reads=["xbc_tok"], store=True)
                p.dma("act", conv_s[:, 0:2, :], st_conv[:, 1:3, :], store=True)
                for c in range(4):
                    col = 3608 + c * 128
                    p.group("pe", [MM(ps[:, 7, c * NS:(c + 1) * NS], w_in_sb[:, kc, col:col + 128], hnTs[:, kc, :],
                                      start=(kc == 0), stop=(kc == 7)) for kc in range(8)],
                            reads=["hnTs"] + WINB, writes=pk(7))
                p.op("act", ACT(vTs[:].rearrange("p c b -> p (c b)"), ps[:, 7, 0:4 * NS], AF.Copy),
                     reads=pk(7), writes=["vTs"])
                p.group("pe", [TR(ps[0:NS, 2, i * 128:(i + 1) * 128], vTs[:, i, :], ident[:]) for i in range(4)],
                        reads=["vTs", "ident"], writes=pk(2))
                p.op("act", ACT(v_tok[:], ps[0:NS, 2, :], AF.Copy), reads=pk(2), writes=["v_tok"])
                p.dma("sp", pool_s[:, 14, :], v_tok[:], reads=["v_tok"], store=True)
                p.dma("act", pool_s[:, 0:14, :], st_pool[:, 1:15, :], store=True)
                for c in range(4):
                    b = 0 + (c % 2)
                    p.group("pe", [TR(ps[:, b, i * 120:(i + 1) * 120], stp_tok[i][0:120, c * 128:(c + 1) * 128],
                                      ident[0:120, 0:120]) for i in range(2)],
                            reads=["stp_tok0", "stp_tok1", "ident"], writes=pk(b))
                    p.op("dve", CP(stpT[:, c, :, :].rearrange("p b j -> p (b j)"), ps[:, b, 0:240]),
                         reads=pk(b), writes=["stpT"])
                for g in range(4):
                    w = 2 << g
                    p.op("dve", lambda e, g=g, w=w: e.tensor_reduce(wsum[:], stpT[:, g, :, 15 - (w - 1):15], AX.X, ALU.add),
                         reads=["stpT"], writes=["wsum"])
                    p.op("dve", TT(wsum[:], wsum[:], vTs[:, g, :], ALU.add), reads=["wsum", "vTs"], writes=["wsum"])
                    p.op("dve", STT(mTs[:, g, :], wsum[:], 1.0 / w, vTs[:, g, :], ALU.mult, ALU.subtract),
                         reads=["wsum", "vTs"], writes=["mTs"])
                for g in range(4):
                    p.op("pe", MM(ps[:, 7, 64 + g * NS:64 + (g + 1) * NS], pw16[:, g, :], mTs[:, g, :]),
                         reads=["pw16", "mTs"], writes=pk(7))
                    p.op("act", ACT(ycatTs[:, 12 + g, :], ps[:, 7, 64 + g * NS:64 + (g + 1) * NS], AF.Copy,
                                    scale=pscale[:, g:g + 1]), reads=pk(7) + ["vecA"], writes=["ycatTs_p"])
                for g3 in range(3):
                    b = 3 + g3
                    p.group("pe", [MM(ps[0:NS, b, :], hnTs[:, kc, :], w_in_sb[:, kc, g3 * 512:(g3 + 1) * 512],
                                      start=(kc == 0), stop=(kc == 7)) for kc in range(8)],
                            reads=["hnTs"] + WINA, writes=pk(b))
                p.op("act", ACT(zs[:], ps[0:NS, 3:6, :].rearrange("p a b -> p (a b)"), AF.Silu),
                     reads=pk(3) + pk(4) + pk(5), writes=["zs"])
                dtA_s, dec_s = sms[:, 0, :], sms[:, 1, :]
                p.op("dve", TT(dtA_s, dt_s[:], a_bc[0:NS, :], ALU.mult), reads=["dt_s", "a_bc"], writes=["dtA_s"])
                p.op("act", ACT(dec_s, dtA_s, AF.Exp), reads=["dtA_s"], writes=["dec_s"])
                p.op("dve", TT(dtx_s[:].rearrange("p (h d) -> p h d", h=H),
                               xbc_tok[:, 0:SSDW].rearrange("p (h d) -> p h d", h=H),
                               bc(dt_s[:].unsqueeze(2), [NS, H, P]), ALU.mult),
                     reads=["xbc_tok", "dt_s"], writes=["dtx_s"])
                p.dma("sp", scr_dtx, dtx_s[:], reads=["dtx_s"], writes=["scr_dtx"])
                p.dma("sp", scr_dec, dec_s, reads=["dec_s"], writes=["scr_dec"])
                for rep in range(2):
                    p.dma("sp", scr_B[:, rep::2, :], xbc_tok[:, 1536:1792].rearrange("p (g n) -> p g n", g=4),
                          reads=["xbc_tok"], writes=["scr_B"])
                    p.dma("sp", scr_C[:, rep::2, :], xbc_tok[:, 1792:2048].rearrange("p (g n) -> p g n", g=4),
                          reads=["xbc_tok"], writes=["scr_C"])
                p.dma("sp", dtx_r[:].rearrange("p s d -> p (s d)"), scr_dtx.rearrange("b (j f) -> (b j) f", j=8),
                      reads=["scr_dtx"], writes=["dtx_r"])
                p.dma("sp", dec_r[:], scr_dec.rearrange("b (j s) -> (b j) s", j=8), reads=["scr_dec"], writes=["dec_r"])
                p.dma("sp", B_r[:], scr_B.rearrange("b j n -> (b j) n"), reads=["scr_B"], writes=["B_r"])
                p.dma("sp", C_r[:], scr_C.rearrange("b j n -> (b j) n"), reads=["scr_C"], writes=["C_r"])
                ssm_in = st_ssm.rearrange("b (j s) f -> (b j) s f", j=8)
                ssm_out = ssm_s.rearrange("b (j s) f -> (b j) s f", j=8)
                stageA_dma(0)
                stageA_norm(0)
                stageA_tr(0, (4, 5))
                p.op("dve", TT(Tb[:, 0, :], B_r[:], C_r[:], ALU.mult), reads=["B_r", "C_r"], writes=["Tb"])
                p.op("dve", lambda e: e.tensor_reduce(cb_r[:], Tb[:, 0, :], AX.X, ALU.add), reads=["Tb"], writes=["cb_r"])
                PC = 16
                its = [(s_, ph) for s_ in range(3) for ph in range(P // PC)]

                def hload(i):
                    s_, ph = its[i]
                    p.dma("sp", Hb[i % 3][:].rearrange("p a n -> p (a n)"),
                          ssm_in[:, s_, ph * PC * NST:(ph + 1) * PC * NST], writes=[f"Hb{i % 3}"])
                hload(0)
                hload(1)
                for i, (s, ph) in enumerate(its):
                    hb = Hb[i % 3]
                    hk = f"Hb{i % 3}"
                    psl = slice(ph * PC, (ph + 1) * PC)
                    p.op("pool", TT(Pb[:], hb[:], bc(C_r[:].unsqueeze(1), [128, PC, NST]), ALU.mult),
                         reads=[hk, "C_r"], writes=["Pb"])
                    p.op("dve", lambda e, s=s, psl=psl: e.tensor_reduce(y_r[:, s, psl], Pb[:], AX.X, ALU.add),
                         reads=["Pb"], writes=["y_r"])
                    tb, tk = Tb2[i % 2], ("Tb" if i % 2 == 0 else "Tb_1")
                    p.op("pool", TT(tb[:], bc(dtx_r[:, s, psl].unsqueeze(2), [128, PC, NST]),
                                    bc(B_r[:].unsqueeze(1), [128, PC, NST]), ALU.mult),
                         reads=["dtx_r", "B_r"], writes=[tk])
                    p.op("dve", STT(hb[:], hb[:], dec_r[:, s:s + 1], tb[:], ALU.mult, ALU.add),
                         reads=[hk, "dec_r", tk], writes=[hk])
                    if i + 2 < len(its):
                        hload(i + 2)
                    p.dma("act", ssm_out[:, s, ph * PC * NST:(ph + 1) * PC * NST], hb[:].rearrange("p a n -> p (a n)"),
                          reads=[hk], store=True)
                p.op("dve", TT(y_r[:], y_r[:], bc(dec_r[:].unsqueeze(2), [128, 3, P]), ALU.mult),
                     reads=["y_r", "dec_r"], writes=["y_r"])
                p.op("dve", STT(y_r[:].rearrange("p s d -> p (s d)"), dtx_r[:].rearrange("p s d -> p (s d)"), cb_r[:, 0:1],
                                y_r[:].rearrange("p s d -> p (s d)"), ALU.mult, ALU.add),
                     reads=["y_r", "dtx_r", "cb_r"], writes=["y_r"])
                p.dma("sp", scr_y.rearrange("b (j f) -> (b j) f", j=8), y_r[:].rearrange("p s d -> p (s d)"),
                      reads=["y_r"], writes=["scr_y"])
                p.dma("sp", y_tok[:], scr_y, reads=["scr_y"], writes=["y_tok"])
                xt3 = xbc_tok[:, 0:SSDW].rearrange("p (h d) -> p h d", h=H)
                p.op("dve", TT(dtx_s[:].rearrange("p (h d) -> p h d", h=H), xt3,
                               bc(D_bc[0:NS, :].unsqueeze(2), [NS, H, P]), ALU.mult),
                     reads=["xbc_tok", "D_bc", "scr_dtx"], writes=["dtx_s"])
                p.op("dve", TT(y_tok[:], y_tok[:], dtx_s[:], ALU.add), reads=["y_tok", "dtx_s"], writes=["y_tok"])
                p.op("dve", TT(y_tok[:], y_tok[:], zs[:], ALU.mult), reads=["y_tok", "zs"], writes=["y_tok"])
                for g in range(4):
                    p.op("act", ACT(junk[0:NS, 0:384], y_tok[:, g * 384:(g + 1) * 384], AF.Square,
                                    accum_out=ssq4[0:NS, g:g + 1]), reads=["y_tok"], writes=["junk", "ssq4"])
                p.op("act", ACT(ssq4[0:NS, :], ssq4[0:NS, :], AF.Ln, scale=1.0 / 384, bias=eps_t[0:NS, :]),
                     reads=["ssq4", "eps"], writes=["ssq4"])
                p.op("act", ACT(ssq4[0:NS, :], ssq4[0:NS, :], AF.Exp, scale=-0.5), reads=["ssq4"], writes=["ssq4"])
                for g in range(4):
                    p.op("act", ACT(y_tok[:, g * 384:(g + 1) * 384], y_tok[:, g * 384:(g + 1) * 384], AF.Copy,
                                    scale=ssq4[0:NS, g:g + 1]), reads=["y_tok", "ssq4"], writes=["y_tok"])
                p.group("pe", [TR(ps[:, 2, i * NS:(i + 1) * NS], y_tok[:, i * 128:(i + 1) * 128], ident[0:NS, 0:NS])
                               for i in range(12)], reads=["y_tok", "ident"], writes=pk(2))
                p.op("dve", TT(ycatTs[:, 0:12, :], ps[:, 2, 0:12 * NS].rearrange("p (k t) -> p k t", k=12),
                               bc(ssdw.unsqueeze(2), [128, 12, NS]), ALU.mult),
                     reads=pk(2) + ["vecA"], writes=["ycatTs_s"])
                p.dma("sp", xs_in[:], x_s, writes=["xs_in"])
                out_proj(ycatTs, slice(0, NS), NS, ["ycatTs_s", "ycatTs_p"], xs_in, "xs_in", SEQ)
                p.flush()

        sample_mixer()
        with contextlib.ExitStack() as stp:
            def sbp(name, shape, dt=F32):
                return stp.enter_context(nc.sbuf_tensor(name, list(shape), dt))
            Rst = [sbp(f"Rst{i}", [128, 4 + T1], BF16) for i in range(2)]
            halo16 = sbp("halo16", [128, 16, 3], BF16)
            cdiag = sbp("cdiag", [128, 16, 128], BF16)
            NT1_ = SEQ // T1
            xsT = [sbp(f"xsT{i}", [128, T1]) for i in range(3)]
            halo = sbp("halo", [128, 16, 3])
            crat = sbp("crat", [128, 4, 16])
            w3p = sbp("w3p", [128, 16])
            x_tok = [sbp("x_tok0", [128, 2, SSDW], BF16)] * 2
            BT16 = sbp("BT16", [128, 2, T1], BF16)
            Cz = sbp("Cz", [128, 4, T1], BF16)
            Bz = sbp("Bz", [128, 2, 4, 128], BF16)
            V = sbp("V", [128, 4, 15 + T1])
            SRES = [sbp(f"SRES{g}", [128, 15 + T1]) for g in range(4)]
            SSCR = [sbp(f"SSCR{i}", [128, 15 + T1]) for i in range(2)]
            D_fin = []
            D_pool = []
            mT16 = sbp("mT16", [128, 4, T1], BF16)
            dt_tok = sbp("dt_tok", [128, 2, H])
            lndt = sbp("lndt", [128, 2, H])
            zsil = [sbp(f"zsil{i}", [128, SSDW], BF16) for i in range(2)]
            Z_ops = []
            sm = sbp("sm", [128, 8, H])
            cdecH = sbp("cdecH", [128, 12])
            dtA16 = sbp("dtA16", [128, 2, H], BF16)
            xw = sbp("xw", [128, SSDW], BF16)
            CBsb = sbp("CBsb", [128, 4, 128], BF16)
            Eh = [sbp(f"Eh{i}", [128, 128], BF16) for i in range(8)]
            Mh = [sbp(f"Mh{i}", [128, 128], BF16) for i in range(8)]
            ybuf = sbp("ybuf", [128, SSDW])
            y1b = sbp("y1b", [128, SSDW], BF16)
            junkB = sbp("junkB", [128, 384], BF16)
            Dmat = sbp("Dmat", [128, H, 128], BF16)
            H32 = sbp("H32", [128, 768])
            H16 = sbp("H16", [128, 768], BF16)
            ycatT = sbp("ycatT", [128, 16, T1], BF16)
            xres = [sbp(f"xres{i}", [128, D]) for i in range(2)]

            p.op("pool", MS(halo16[:], 0.0), writes=[f"halo{c}" for c in range(16)])
            w3raw = vecA[:, 64:80]
            p.op("dve", TS(crat[:, 3, :], w3raw, 0.0, 1e-12, ALU.is_equal, ALU.mult), reads=["vecA"], writes=["crat"])
            p.op("dve", TT(w3p[:], crat[:, 3, :], w3raw, ALU.add), reads=["crat", "vecA"], writes=["crat"])
            p.op("dve", lambda e: e.reciprocal(crat[:, 3, :], w3p[:]), reads=["crat"], writes=["crat"])
            for k in range(3):
                p.op("dve", TT(crat[:, k, :], vecA[:, 16 + k * 16:32 + k * 16], crat[:, 3, :], ALU.mult),
                     reads=["crat", "vecA"], writes=["crat"])
            for c in range(16):
                p.op("dve", TS(cdiag[:, c, :], ident[:], crat[:, 2, c:c + 1], None, ALU.mult),
                     reads=["ident", "crat"], writes=["cdiag"])
            p.op("pool", MS(V[:], 0.0), writes=[f"V{c}" for c in range(4)])
            p.op("pool", MS(Cz[:], 0.0), writes=["Cz"])
            p.op("pool", MS(Bz[:], 0.0), writes=["Bz"])
            p.op("pool", MS(H32[:], 0.0), writes=["H32"])
            p.op("pool", MS(H16[:], 0.0), writes=["H16"])
            for h in range(H):
                p.op("dve", TS(Dmat[:, h, :], ident[:], D_bc[:, h:h + 1], None, ALU.mult),
                     reads=["ident", "D_bc"], writes=["Dmat"])


            NT1 = SEQ // T1
            stageA_dma(1)
            stageA_norm(1)
            for ti in range(NT1):
                stage_BCD(ti, list(range(16)))

                def gap0(ti=ti):
                    if ti + 1 < NT1:
                        stageA_tr(ti + 1, (6, 7))
                    if ti + 2 < NT1:
                        stageA_dma(ti + 2)
                if ti == NT1 - 1:
                    rows_out(lambda c: halo[:, c, :], 16, 3, conv_p, xin[0], [f"halo32_{c}" for c in range(16)], skey="xin0")
                    rows_out(lambda c: V[:, c, T1:T1 + 15], 4, 15, pool_p, xin[0], [f"V{c}" for c in range(4)], skey="xin0")
                ssd_chunk(ti, 0, gap=gap0, defer_out=True, early_next=lambda ti=ti: chunk_prologue(ti, 1))
                def gap1(ti=ti):
                    do_out(ti, 0, (6, 7))
                    if ti == NT1 - 1:
                        Hout = xin[1][:, 0:768].rearrange("p (q x) -> p q x", q=6)
                        for q in range(6):
                            b = 6 + (q % 2)
                            p.op("pe", TR(ps[:, b, 0:128], H32[:, q * 128:(q + 1) * 128], ident[:]), reads=["H32", "ident"],
                                 writes=pk(b))
                            p.op("act", ACT(Hout[:, q, :], ps[:, b, 0:128], AF.Copy), reads=pk(b), writes=["xin1"])
                            jj, rp = q // 3, (q % 3) * 2
                            for a in range(2):
                                h0 = 12 * jj + 6 * a + rp
                                p.dma("sp", ssm_p[h0:h0 + 2].rearrange("h p n -> (h p) n"), Hout[:, q, a * 64:(a + 1) * 64],
                                      reads=["xin1"], store=True)
                ssd_chunk(ti, 1, skip_prologue=True, gap=gap1,
                          pre_heads=(lambda ti=ti: stageA_norm(ti + 2)) if ti + 2 < NT1 else None)
            p.flush()

    with contextlib.ExitStack() as st2:
        def sb2(name, shape, dt=F32):
            return st2.enter_context(nc.sbuf_tensor(name, list(shape), dt))

        w_up_sb = sb2("w_up_sb", [128, 8, 2 * DFF], BF16)
        w_dn_sb = sb2("w_dn_sb", [128, 22, D], BF16)
        w_up_v = w_up.rearrange("(k p) c -> p k c", p=128)
        halo2 = sb2("halo2", [128, 44, 2])
        p.op("pool", MS(halo2[:], 0.0), writes=[f"halo2_{c}" for c in range(44)])
        blocks = []
        for g4 in range(6):
            lo, hi = 4 * g4, min(22, 4 * g4 + 4)
            blocks += [(lo, hi), (22 + lo, 22 + hi)]
        for bi, (lo, hi) in enumerate(blocks):
            p.dma("pool", w_up_sb[:, :, lo * 128:hi * 128], w_up_v[:, :, lo * 128:hi * 128],
                  writes=[f"w_up{cc}" for cc in range(lo, hi)])
        for fc in range(22):
            p.dma("pool", w_dn_sb[:, fc, :], w_down[fc * 128:(fc + 1) * 128, :], writes=[f"w_dn{fc}"])
        WDN = [f"w_dn{fc}" for fc in range(22)]

        ssq2 = sb2("ssq2", [128, 8])
        xin2 = [sb2(f"xin2_{i}", [128, D]) for i in range(2)]
        junk2 = sb2("junk2", [128, D], BF16)
        hn2T = sb2("hn2T", [128, 8, T2], BF16)
        RR = sb2("RR", [128, 4, 2 + T2])
        Rg = [RR[:, i, :] for i in range(2)]
        Rv = [RR[:, 2 + i, :] for i in range(2)]
        xb = [xin2[0], xin2[1], RR[:, 0:2, :].rearrange("p a b -> p (a b)")[:, 0:D],
              RR[:, 2:4, :].rearrange("p a b -> p (a b)")[:, 0:D]]
        xbk = [["xin2_0"], ["xin2_1"], ["Rg0", "Rg0h", "Rg1", "Rg1h"], ["Rv0", "Rv0h", "Rv1", "Rv1h"]]
        ag = [sb2(f"ag{i}", [128, T2]) for i in range(2)]
        actT = sb2("actT", [128, 22, T2], BF16)
        stf_tok = [sb2("stf_tok0", [32, 4 * 128])] * 2
        stfT = sb2("stfT", [128, 44, NS, 2])
        uraw = sb2("uraw", [128, 44, NS])
        rst2 = [sb2(f"rst2_{i}", [16, 512]) for i in range(2)]

        def subtiles(ntok):
            return [(j, min(128, ntok - j * 128)) for j in range((ntok + 127) // 128)]

        def ffn_norm_front(row0, ntok):
            for j, nr in subtiles(ntok):
                xt, xk = xb[j], xbk[j]
                p.dma("sp", xt[0:nr, :], x2s[row0 + j * 128: row0 + j * 128 + nr, :],
                      reads=[f"x2s{row0 + j * 128}"], writes=xk)
                sq = ssq2[0:nr, j:j + 1]
                p.op("act", ACT(junk2[0:nr, :], xt[0:nr, :], AF.Square, accum_out=sq),
                     reads=xk, writes=["junk2", f"ssq2_{j}"])
                rstd_from_ssq(sq, D, nr, f"ssq2_{j}")
                p.op("act", ACT(xt[0:nr, :], xt[0:nr, :], AF.Copy, scale=sq),
                     reads=xk + [f"ssq2_{j}"], writes=xk)

        def ffn_norm_back(ntok, banks=(0, 1, 2, 3)):
            for j, nr in subtiles(ntok):
                xt, xk = xb[j], xbk[j]
                for half in range(2):
                    b = banks[(j % 2) * 2 + half]
                    p.group("pe", [TR(ps[:, b, i * nr:(i + 1) * nr],
                                      xt[0:nr, (half * 4 + i) * 128:(half * 4 + i + 1) * 128],
                                      ident[0:nr, 0:nr]) for i in range(4)],
                            reads=xk + ["ident"], writes=pk(b))
                    p.op("dve", TT(hn2T[:, half * 4:half * 4 + 4, j * 128:j * 128 + nr],
                                   ps[:, b, 0:4 * nr].rearrange("p (k t) -> p k t", k=4),
                                   bc(n2w[:, half * 4:half * 4 + 4].unsqueeze(2), [128, 4, nr]), ALU.mult),
                         reads=pk(b) + ["vecA"], writes=[f"hn2T{j}"])

        def ffn_up(ntok, sample=False, extra=None):
            hkeys = [f"hn2T{j}" for j, _ in subtiles(ntok)]
            pend = []
            for fc in range(22):
                i2 = fc % 2
                gb = fc % 3
                vb = 3 + fc % 3
                for (cc, R, rk, bank) in ((fc, Rg[i2], f"Rg{i2}", gb), (22 + fc, Rv[i2], f"Rv{i2}", vb)):
                    accp = ps[:, bank, 0:ntok]
                    p.group("pe", [MM(accp, w_up_sb[:, kc, cc * 128:(cc + 1) * 128], hn2T[:, kc, 0:ntok],
                                      start=(kc == 0), stop=(kc == 7)) for kc in range(8)],
                            reads=hkeys + [f"w_up{cc}"], writes=pk(bank))
                    if not sample:
                        p.op("pool", CP(R[:, 0:2], halo2[:, cc, :]), reads=[f"halo2_{cc}"], writes=[rk + "h"])
                        p.op("act", ACT(R[:, 2:2 + ntok], accp, AF.Copy), reads=pk(bank), writes=[rk])
                        p.op("act", ACT(accp, accp, AF.Identity, scale=fw(2, cc), bias=fb(cc)),
                             reads=pk(bank) + ["vecB"], writes=pk(bank))
                        for k in range(2):
                            p.op("dve", STT(accp, R[:, k:k + ntok], fw(k, cc), accp, ALU.mult, ALU.add),
                                 reads=[rk, rk + "h", "vecB"] + pk(bank), writes=pk(bank))
                        p.op("pool", CP(halo2[:, cc, :], R[:, ntok:ntok + 2]), reads=[rk], writes=[f"halo2_{cc}"])
                    else:
                        p.op("act", ACT(uraw[:, cc, :], accp, AF.Copy), reads=pk(bank), writes=["uraw"])
                        p.op("act", ACT(accp, accp, AF.Identity, scale=fw(2, cc), bias=fb(cc)),
                             reads=pk(bank) + ["vecB"], writes=pk(bank))
                        for k in range(2):
                            p.op("dve", STT(accp, stfT[:, cc, :, k], fw(k, cc), accp,
                                            ALU.mult, ALU.add), reads=["stfT", "vecB"] + pk(bank), writes=pk(bank))

                def fin(fc=fc, i2=i2, gb=gb, vb=vb):
                    p.op("act", ACT(ag[i2][:, 0:ntok], ps[:, gb, 0:ntok], AF.Silu), reads=pk(gb), writes=[f"ag{i2}"])
                    p.op("dve", TT(actT[:, fc, 0:ntok], ag[i2][:, 0:ntok], ps[:, vb, 0:ntok], ALU.mult),
                         reads=[f"ag{i2}"] + pk(vb), writes=[f"actT{fc}"])
                pend.append(fin)
                if len(pend) > 1:
                    pend.pop(0)()
                if extra:
                    extra.pop(0)()
            while pend:
                pend.pop(0)()
            while extra:
                extra.pop(0)()

        def ffn_up_sample():
            hkeys = ["hn2T0"]
            for half, bank in ((0, 0), (1, 3)):
                for fc in range(22):
                    cc = half * 22 + fc
                    p.group("pe", [MM(ps[:, bank, fc * NS:(fc + 1) * NS], w_up_sb[:, kc, cc * 128:(cc + 1) * 128],
                                      hn2T[:, kc, 0:NS], start=(kc == 0), stop=(kc == 7)) for kc in range(8)],
                            reads=hkeys + [f"w_up{cc}"], writes=pk(bank))
            accs = []
            for half, bank, A, akk, Tm, tk in ((0, 0, Rg[0], "Rg0", Rg[1], "Rg1"), (1, 3, Rv[0], "Rv0", Rv[1], "Rv1")):
                src = ps[:, bank, 0:22 * NS].rearrange("p (c b) -> p c b", c=22)
                ur = uraw[:, half * 22:(half + 1) * 22, :]
                acc = A[:, 0:22 * NS].rearrange("p (c b) -> p c b", c=22)
                tmp = Tm[:, 0:22 * NS].rearrange("p (c b) -> p c b", c=22)

                def wb(k, half=half):
                    return bc(vecB[:, k * 44 + half * 22:k * 44 + half * 22 + 22].unsqueeze(2), [128, 22, NS])
                p.op("act", ACT(ur, src, AF.Copy), reads=pk(bank), writes=["uraw"])
                p.op("dve", TT(acc, src, wb(2), ALU.mult), reads=pk(bank) + ["vecB"], writes=[akk])
                p.op("dve", TT(acc, acc, wb(3), ALU.add), reads=[akk, "vecB"], writes=[akk])
                for k in range(2):
                    p.op("dve", TT(tmp, stfT[:, half * 22:(half + 1) * 22, :, k], wb(k), ALU.mult),
                         reads=["stfT", "vecB"], writes=[tk])
                    p.op("dve", TT(acc, acc, tmp, ALU.add), reads=[akk, tk], writes=[akk])
                accs.append((acc, akk))
            sg = ag[0][:, 0:22 * NS].rearrange("p (c b) -> p c b", c=22)
            p.op("act", ACT(sg, accs[0][0], AF.Silu), reads=[accs[0][1]], writes=["ag0"])
            p.op("dve", TT(actT[:, :, 0:NS], sg, accs[1][0], ALU.mult), reads=["ag0", accs[1][1]],
                 writes=[f"actT{fc}" for fc in range(22)])

        def ffn_down(row0, ntok, sample=False):
            AK = [f"actT{fc}" for fc in range(22)]
            for j, nr in subtiles(ntok):
                r0 = row0 + j * 128
                for hh in range(2):
                    p.dma("sp", ag[hh][0:nr, :], x2s[r0:r0 + nr, hh * 512:(hh + 1) * 512],
                          reads=[f"x2s{r0}"], writes=[f"ag{hh}"])
                for hh in range(2):
                    b = 6 + hh
                    p.group("pe", [MM(ps[0:nr, b, :], actT[:, fc, j * 128:j * 128 + nr], w_dn_sb[:, fc, hh * 512:(hh + 1) * 512],
                                      start=(fc == 0), stop=False) for fc in range(21)],
                            reads=AK[0:21] + WDN[0:21], writes=pk(b))
                    p.op("pe", MM(ps[0:nr, b, :], actT[:, 21, j * 128:j * 128 + nr], w_dn_sb[:, 21, hh * 512:(hh + 1) * 512],
                                  start=False, stop=True), reads=AK[21:] + WDN[21:], writes=pk(b))
                    p.op("dve", TT(ag[hh][0:nr, :], ps[0:nr, b, :], ag[hh][0:nr, :], ALU.add),
                         reads=pk(b) + [f"ag{hh}"], writes=[f"ag{hh}"])
                    p.op("act", ACT(junk2[0:nr, 0:512], ag[hh][0:nr, :], AF.Square, accum_out=ssq2[0:nr, 4 + hh:5 + hh]),
                         reads=[f"ag{hh}"], writes=["junk2", f"ssq2b{hh}"])
                sq = ssq2[0:nr, 6:7]
                p.op("dve", TT(sq, ssq2[0:nr, 4:5], ssq2[0:nr, 5:6], ALU.add), reads=["ssq2b0", "ssq2b1"], writes=["ssq2c"])
                rstd_from_ssq(sq, D, nr, "ssq2c")
                for hh in range(2):
                    p.op("dve", STT(ag[hh][0:nr, :], ag[hh][0:nr, :], sq, fnw_bc[0:nr, hh * 512:(hh + 1) * 512],
                                    ALU.mult, ALU.mult), reads=[f"ag{hh}", "ssq2c", "fnw_bc"], writes=[f"ag{hh}"])
                    dst = (y_s if sample else y_p[r0:r0 + nr, :])[:, hh * 512:(hh + 1) * 512]
                    p.dma("sp", dst, ag[hh][0:nr, :], reads=[f"ag{hh}"], store=True)

        stf_flat = st_ffn.rearrange("b k c -> (b k) c")
        stf_ops = []
        for pc in range(11):
            def s_dma(pc=pc):
                p.dma("sp", stf_tok[0][:], stf_flat[:, pc * 512:(pc + 1) * 512], writes=["stf_tok0"])

            def s_tr(pc=pc):
                b = 6 + (pc % 2)
                p.group("pe", [TR(ps[:, b, i * 32:(i + 1) * 32], stf_tok[0][0:32, i * 128:(i + 1) * 128],
                                  ident[0:32, 0:32]) for i in range(4)], reads=["stf_tok0", "ident"], writes=pk(b))
                p.op("dve", CP(stfT[:, pc * 4:pc * 4 + 4, :, :].rearrange("p c b k -> p c (b k)"),
                               ps[:, b, 0:128].rearrange("p (c x) -> p c x", c=4)), reads=pk(b), writes=["stfT"])
            stf_ops += [s_dma, s_tr]
        NT2 = SEQ // T2
        ffn_norm_front(0, T2)
        ffn_norm_back(T2)
        for t in range(NT2):
            ffn_up(T2, extra=stf_ops if t == 2 else None)
            if t + 1 == NT2:
                for k in range(2):
                    p.dma("act", ffn_p[k:k + 1, :].rearrange("o (c q) -> q (o c)", q=128), halo2[:, :, k],
                          reads=[f"halo2_{c}" for c in range(44)], store=True, allow_slow_non_contiguous=True)
            if t + 1 < NT2:
                ffn_norm_front((t + 1) * T2, T2)
            else:
                ffn_norm_front(SEQ, NS)
            ffn_down(t * T2, T2)
            ffn_norm_back(T2 if t + 1 < NT2 else NS)
        ffn_up_sample()
        p.dma("act", ffn_s[:, 0, :], st_ffn[:, 1, :], store=True)
        rows_out(lambda c: uraw[:, c, :], 44, NS, ffn_s[:, 1, :], rst2, ["uraw"], banks=(2, 4, 5), skey="rst2")
        ffn_down(SEQ, NS, sample=True)
        p.flush(final=True)
    gst.close()
    return nc


_NC_CACHE = {}


def kernel(x_prompt, x_sample, state_ssm, state_conv, state_pool, state_ffn_conv,
           norm1_w, w_in, conv_w, conv_b, dt_bias, a_log, d_skip, ssd_norm_w,
           pool_w, pool_scale, w_out, norm2_w, w_up, ffn_conv_w, ffn_conv_b, w_down,
           final_norm_w):
    f = lambda a: np.ascontiguousarray(np.asarray(a, dtype=np.float32))
    if "nc" not in _NC_CACHE:
        _NC_CACHE["nc"] = build_nc()
    nc = _NC_CACHE["nc"]
    shared = {
        "norm1_w": f(norm1_w).reshape(8, 128),
        "w_in": f(w_in)[0],
        "conv_w": f(conv_w).reshape(64, 128),
        "conv_b": f(conv_b).reshape(16, 128),
        "dt_bias": f(dt_bias).reshape(1, H),
        "a_log": f(a_log).reshape(1, H),
        "d_skip": f(d_skip).reshape(1, H),
        "ssd_norm_w": f(ssd_norm_w).reshape(12, 128),
        "pool_w": f(pool_w)[0],
        "pool_scale": f(pool_scale).reshape(4, 128),
        "w_out": f(w_out)[0],
        "norm2_w": f(norm2_w).reshape(8, 128),
        "w_up": f(w_up)[0],
        "ffn_conv_w": f(ffn_conv_w).reshape(132, 128),
        "ffn_conv_b": f(ffn_conv_b).reshape(44, 128),
        "w_down": f(w_down)[0],
        "final_norm_w": f(final_norm_w).reshape(1, D),
    }
    xp, xs = f(x_prompt), f(x_sample)
    s_ssm, s_conv, s_pool, s_ffn = f(state_ssm)[0], f(state_conv)[0], f(state_pool)[0], f(state_ffn_conv)[0]
    in_maps = []
    for c in range(NCORES):
        sl = slice(c * NS, (c + 1) * NS)
        m = dict(shared)
        m["x_p"] = xp[c]
        m["x_s"] = xs[sl, 0, :]
        m["st_ssm"] = s_ssm[sl].reshape(NS, H, P * NST)
        m["st_conv"] = s_conv[sl]
        m["st_pool"] = s_pool[sl]
        m["st_ffn"] = s_ffn[sl]
        in_maps.append(m)
    res = run_bass_kernel_spmd(nc, in_maps, core_ids=list(range(NCORES)))
    R = res.results
    cat = lambda k: np.concatenate([np.asarray(r[k]) for r in R], axis=0)
    stk = lambda k: np.stack([np.asarray(r[k]) for r in R], axis=0)
    y_prompt = stk("y_p")
    y_sample = cat("y_s").reshape(NCORES * NS, 1, D)
    ssm_prompt = stk("ssm_p")[None]
    ssm_sample = cat("ssm_s").reshape(1, NCORES * NS, H, P, NST)
    conv_prompt = stk("conv_p")[None]
    conv_sample = cat("conv_s")[None]
    pool_prompt = stk("pool_p")[None]
    pool_sample = cat("pool_s")[None]
    ffn_prompt = stk("ffn_p")[None]
    ffn_sample = cat("ffn_s")[None]
    return (y_prompt, y_sample, ssm_prompt, ssm_sample, conv_prompt, conv_sample,
            pool_prompt, pool_sample, ffn_prompt, ffn_sample)
```

```python
import contextlib
import numpy as np
import concourse.bass as bass
import concourse.mybir as mybir
from concourse.bass_utils import run_bass_kernel_spmd

F32 = mybir.dt.float32
BF16 = mybir.dt.bfloat16
AF = mybir.ActivationFunctionType
ALU = mybir.AluOpType
AX = mybir.AxisListType

NCORES = 8
D = 1024
SEQ = 2048
NS = 16
H = 24
P = 64
NST = 64
G = 4
SSDW = 1536
CONVD = 2048
POOLW = 512
DFF = 2816
INC = 4120
EPS = 1e-6
T1 = 256
T2 = 512
NEGBIG = -30000.0

ENGS = ["pe", "act", "dve", "pool", "sp"]


class Prog:
    def __init__(self, nc, n_dma_sems=10):
        self.nc = nc
        self.stack = contextlib.ExitStack()
        self.ops = {e: [] for e in ENGS}
        self.cnt = {e: 0 for e in ENGS}
        self.sem = {}
        for e in ENGS[:4]:
            self.sem[e] = self.stack.enter_context(nc.semaphore("s_" + e))
        self.dma_sems = {}
        self.dma_sem_val = {}
        self.dma_rr = {}
        for q, n in (("sp", 14), ("act", 6), ("pool", 44)):
            self.dma_sems[q] = [self.stack.enter_context(nc.semaphore(f"d_{q}_{i}"))
                                for i in range(n)]
            self.dma_rr[q] = 0
        self.waited = {}
        self.lastw = {}
        self.readers = {}
        self.store_tokens = []

    def _need(self, eng, tok, waits):
        sem, val, teng = tok
        if teng == eng and eng == "pe":
            return
        k = (eng, id(sem))
        if self.waited.get(k, 0) >= val:
            return
        self.waited[k] = val
        waits.append((sem, val))

    def _deps(self, eng, reads, writes):
        waits = []
        for k in reads:
            t = self.lastw.get(k)
            if t is not None:
                self._need(eng, t, waits)
            if k.startswith("ps"):
                for t in self.readers.get(k, ()):
                    if t[2] != eng:
                        self._need(eng, t, waits)
        for k in writes:
            t = self.lastw.get(k)
            if t is not None:
                self._need(eng, t, waits)
            for t in self.readers.get(k, ()):
                self._need(eng, t, waits)
        return waits

    def _commit(self, tok, reads, writes):
        for k in reads:
            self.readers.setdefault(k, []).append(tok)
        for k in writes:
            self.lastw[k] = tok
            self.readers[k] = []

    def op(self, eng, fn, reads=(), writes=()):
        waits = self._deps(eng, reads, writes)
        self.cnt[eng] += 1
        tok = (self.sem[eng], self.cnt[eng], eng)
        self.ops[eng].append((waits, fn, (self.sem[eng], 1)))
        self._commit(tok, reads, writes)
        return tok

    def group(self, eng, fns, reads=(), writes=()):
        waits = self._deps(eng, reads, writes)
        self.cnt[eng] += 1
        tok = (self.sem[eng], self.cnt[eng], eng)
        n = len(fns)
        for i, fn in enumerate(fns):
            self.ops[eng].append((waits if i == 0 else [], fn,
                                  (self.sem[eng], 1) if i == n - 1 else None))
        self._commit(tok, reads, writes)
        return tok

    def dma(self, q, out, in_, reads=(), writes=(), store=False, **kw):
        waits = self._deps(q, reads, writes)
        i = self.dma_rr[q]
        self.dma_rr[q] = (i + 1) % len(self.dma_sems[q])
        sem = self.dma_sems[q][i]
        prev = self.dma_sem_val.get(id(sem), 0)
        if prev > 0:
            self._need(q, (sem, prev, None), waits)
        val = prev + 16
        self.dma_sem_val[id(sem)] = val
        tok = (sem, val, None)

        def fn(e, out=out, in_=in_, kw=kw):
            return e.dma_start(out=out, in_=in_, **kw)
        self.ops[q].append((waits, fn, (sem, 16)))
        self._commit(tok, reads, writes)
        if store:
            self.store_tokens.append(tok)
        return tok

    def flush(self, final=False):
        if final:
            waits = []
            for t in self.store_tokens:
                self._need("sp", t, waits)
            for e in ENGS[:4]:
                if self.cnt[e] > 0:
                    self._need("sp", (self.sem[e], self.cnt[e], e), waits)
            self.ops["sp"].append((waits, None, None))
        else:
            for e in ENGS:
                waits = []
                for e2 in ENGS[:4]:
                    if e2 != e and self.cnt[e2] > 0:
                        self._need(e, (self.sem[e2], self.cnt[e2], e2), waits)
                for q in self.dma_sems:
                    for sem in self.dma_sems[q]:
                        v = self.dma_sem_val.get(id(sem), 0)
                        if v > 0:
                            self._need(e, (sem, v, None), waits)
                self.ops[e].append((waits, None, None))
        ops = self.ops
        self.ops = {e: [] for e in ENGS}

        def replay(e, lst):
            for waits, fn, inc in lst:
                for sem, val in waits:
                    e.wait_ge(sem, val)
                if fn is None:
                    continue
                ins = fn(e)
                if inc is not None:
                    ins.then_inc(inc[0], inc[1])

        with self.nc.Block() as block:
            @block.tensor
            def _(e):
                replay(e, ops["pe"])

            @block.scalar
            def _(e):
                replay(e, ops["act"])

            @block.vector
            def _(e):
                replay(e, ops["dve"])

            @block.gpsimd
            def _(e):
                replay(e, ops["pool"])

            @block.sync
            def _(e):
                replay(e, ops["sp"])


def MM(out, lhsT, rhs, start=True, stop=True):
    return lambda e: e.matmul(out, lhsT, rhs, start=start, stop=stop)


def TR(out, in_, ident):
    return lambda e: e.transpose(out, in_, ident)


def ACT(out, in_, func, **kw):
    return lambda e: e.activation(out, in_, func, **kw)


def TT(out, a, b, op):
    return lambda e: e.tensor_tensor(out, a, b, op)


def TS(out, a, s1, s2, op0, op1=None):
    if op1 is None:
        return lambda e: e.tensor_scalar(out, a, s1, s2, op0)
    return lambda e: e.tensor_scalar(out, a, s1, s2, op0, op1)


def STT(out, in0, scalar, in1, op0, op1):
    return lambda e: e.scalar_tensor_tensor(out, in0, scalar, in1, op0, op1)


def CP(out, in_):
    return lambda e: e.tensor_copy(out, in_)


def MS(out, val):
    return lambda e: e.memset(out, val)


def pk(b, q0=0, q1=4):
    return [f"ps{b}"]


def bc(ap, shape):
    return ap.to_broadcast(list(shape))


def build_nc():
    nc = bass.Bass("TRN2", target_bir_lowering=False)

    def dram(name, shape, kind, dt=F32):
        return nc.dram_tensor(name, list(shape), dt, kind=kind).ap()

    EI, EO = "ExternalInput", "ExternalOutput"
    x_p = dram("x_p", [SEQ, D], EI)
    x_s = dram("x_s", [NS, D], EI)
    st_ssm = dram("st_ssm", [NS, H, P * NST], EI)
    st_conv = dram("st_conv", [NS, 3, CONVD], EI)
    st_pool = dram("st_pool", [NS, 15, POOLW], EI)
    st_ffn = dram("st_ffn", [NS, 2, 2 * DFF], EI)
    norm1_w = dram("norm1_w", [D // 128, 128], EI)
    w_in = dram("w_in", [D, INC], EI)
    conv_w = dram("conv_w", [4 * 16, 128], EI)
    conv_b = dram("conv_b", [16, 128], EI)
    dt_bias = dram("dt_bias", [1, H], EI)
    a_log = dram("a_log", [1, H], EI)
    d_skip = dram("d_skip", [1, H], EI)
    ssd_norm_w = dram("ssd_norm_w", [12, 128], EI)
    pool_w = dram("pool_w", [4, 128, 128], EI)
    pool_scale = dram("pool_scale", [4, 128], EI)
    w_out = dram("w_out", [2 * D, D], EI)
    norm2_w = dram("norm2_w", [D // 128, 128], EI)
    w_up = dram("w_up", [D, 2 * DFF], EI)
    ffn_conv_w = dram("ffn_conv_w", [3 * 44, 128], EI)
    ffn_conv_b = dram("ffn_conv_b", [44, 128], EI)
    w_down = dram("w_down", [DFF, D], EI)
    final_norm_w = dram("final_norm_w", [1, D], EI)

    y_p = dram("y_p", [SEQ, D], EO)
    y_s = dram("y_s", [NS, D], EO)
    ssm_p = dram("ssm_p", [H, P, NST], EO)
    ssm_s = dram("ssm_s", [NS, H, P * NST], EO)
    conv_p = dram("conv_p", [3, CONVD], EO)
    conv_s = dram("conv_s", [NS, 3, CONVD], EO)
    pool_p = dram("pool_p", [15, POOLW], EO)
    pool_s = dram("pool_s", [NS, 15, POOLW], EO)
    ffn_p = dram("ffn_p", [2, 2 * DFF], EO)
    ffn_s = dram("ffn_s", [NS, 2, 2 * DFF], EO)

    x2s = dram("x2s", [SEQ + NS, D], "Internal")
    scr_dtx = dram("scr_dtx", [NS, SSDW], "Internal")
    scr_dec = dram("scr_dec", [NS, H], "Internal")
    scr_B = dram("scr_B", [NS, 8, 64], "Internal")
    scr_C = dram("scr_C", [NS, 8, 64], "Internal")
    scr_y = dram("scr_y", [NS, SSDW], "Internal")

    p = Prog(nc)
    gst = p.stack

    def sbg(name, shape, dt=F32):
        return gst.enter_context(nc.sbuf_tensor(name, list(shape), dt))

    ps = gst.enter_context(nc.psum_tensor("ps", [128, 8, 512], F32))

    ident = sbg("ident", [128, 128])
    identb = sbg("identb", [128, 128], BF16)
    tri = sbg("tri", [128, 128])
    ones = sbg("ones", [128, 128])
    negtri = sbg("negtri", [128, 128], BF16)
    tri16 = sbg("tri16", [128, 128], BF16)
    vecA = sbg("vecA", [128, 112])
    vecB = sbg("vecB", [128, 176])
    a_bc = sbg("a_bc", [128, H])
    dtb_bc = sbg("dtb_bc", [128, H])
    D_bc = sbg("D_bc", [128, H])
    fnw_bc = sbg("fnw_bc", [128, D])
    invc = sbg("invc", [128, 16])
    stage_v = sbg("stage_v", [128, 128])
    eps_t = sbg("eps_t", [128, 1])

    n1w = vecA[:, 0:8]
    n2w = vecA[:, 8:16]
    ssdw = vecA[:, 96:108]
    pscale = vecA[:, 108:112]

    def cw(k, c):
        return vecA[:, 16 + k * 16 + c:17 + k * 16 + c]

    def cb(c):
        return vecA[:, 80 + c:81 + c]

    def fw(k, c):
        return vecB[:, k * 44 + c:k * 44 + c + 1]

    def fb(c):
        return vecB[:, 132 + c:133 + c]

    p.op("pool", MS(ident[:], 1.0), writes=["ident"])
    p.op("pool", lambda e: e.affine_select(ident[:], ident[:], [[-1, 128]], ALU.is_equal, 0.0,
                                            base=0, channel_multiplier=1), reads=["ident"], writes=["ident"])
    p.op("pool", MS(tri[:], 1.0), writes=["tri"])
    p.op("pool", lambda e: e.affine_select(tri[:], tri[:], [[1, 128]], ALU.is_ge, 0.0,
                                            base=0, channel_multiplier=-1), reads=["tri"], writes=["tri"])
    p.op("pool", MS(ones[:], 1.0), writes=["ones"])
    p.op("pool", MS(negtri[:], NEGBIG), writes=["negtri"])
    p.op("pool", lambda e: e.affine_select(negtri[:], negtri[:], [[-1, 128]], ALU.is_gt, 0.0,
                                            base=0, channel_multiplier=1), reads=["negtri"], writes=["negtri"])
    p.op("dve", CP(identb[:], ident[:]), reads=["ident"], writes=["identb"])
    p.op("dve", CP(tri16[:], tri[:]), reads=["tri"], writes=["tri16"])
    p.op("dve", MS(eps_t[:], EPS), writes=["eps"])
    p.op("pool", lambda e: e.iota(invc[:], [[1, 16]], base=1, channel_multiplier=0,
                                  allow_small_or_imprecise_dtypes=True), writes=["invc"])
    p.op("dve", lambda e: e.reciprocal(invc[:], invc[:]), reads=["invc"], writes=["invc"])

    def load_vec_T(dst, pieces, tag):
        r0 = 0
        for src, nrows in pieces:
            p.dma("sp", stage_v[r0:r0 + nrows, :], src, writes=["stage_v"])
            r0 += nrows
        p.op("pe", TR(ps[:, 7, 0:r0], stage_v[0:r0, :], ident[0:r0, 0:r0]),
             reads=["stage_v", "ident"], writes=pk(7))
        p.op("dve", CP(dst, ps[:, 7, 0:r0]), reads=pk(7), writes=[tag])

    load_vec_T(vecA[:, 0:112], [(norm1_w, 8), (norm2_w, 8), (conv_w, 64), (conv_b, 16),
                                (ssd_norm_w, 12), (pool_scale, 4)], "vecA")
    load_vec_T(vecB[:, 0:88], [(ffn_conv_w[0:88, :], 88)], "vecB")
    load_vec_T(vecB[:, 88:176], [(ffn_conv_w[88:132, :], 44), (ffn_conv_b, 44)], "vecB")

    p.dma("sp", a_bc[:], a_log.partition_broadcast(128), writes=["a_bc"])
    p.dma("sp", dtb_bc[:], dt_bias.partition_broadcast(128), writes=["dtb_bc"])
    p.dma("sp", D_bc[:], d_skip.partition_broadcast(128), writes=["D_bc"])
    p.dma("sp", fnw_bc[:], final_norm_w.partition_broadcast(128), writes=["fnw_bc"])
    p.op("act", ACT(a_bc[:], a_bc[:], AF.Exp), reads=["a_bc"], writes=["a_bc"])
    p.op("dve", TS(a_bc[:], a_bc[:], -1.0, None, ALU.mult), reads=["a_bc"], writes=["a_bc"])

    def rstd_from_ssq(ssq, n, nrows, tag):
        p.op("act", ACT(ssq, ssq, AF.Ln, scale=1.0 / n, bias=eps_t[0:nrows, :]), reads=[tag, "eps"], writes=[tag])
        p.op("act", ACT(ssq, ssq, AF.Exp, scale=-0.5), reads=[tag], writes=[tag])

    def rows_out(src_fn, C, R, dst, stage, tag, banks=(6, 7), skey="junk"):
        bi = 0
        for c0 in range(0, C, 4):
            cn = min(4, C - c0)
            b = banks[bi % len(banks)]
            bi += 1
            p.group("pe", [TR(ps[0:R, b, i * 128:(i + 1) * 128], src_fn(c0 + i), ident[:, :])
                           for i in range(cn)], reads=list(tag) + ["ident"], writes=pk(b))
            if isinstance(stage, list):
                stg, sk = stage[(bi - 1) % len(stage)], f"{skey}{(bi - 1) % len(stage)}"
            else:
                stg, sk = stage, skey
            p.op("act", ACT(stg[0:R, 0:cn * 128], ps[0:R, b, 0:cn * 128], AF.Copy),
                 reads=pk(b), writes=[sk])
            p.dma("sp", dst[:, c0 * 128:(c0 + cn) * 128], stg[0:R, 0:cn * 128],
                  reads=[sk], store=True)

    with contextlib.ExitStack() as st1:
        def sb1(name, shape, dt=F32):
            return st1.enter_context(nc.sbuf_tensor(name, list(shape), dt))

        WA = sb1("WA", [128, 8 * INC + 16 * D], BF16)
        w_in_sb = WA[:, 0:8 * INC].rearrange("p (k c) -> p k c", k=8)
        w_out_sb = WA[:, 8 * INC:8 * INC + 16 * D].rearrange("p (k c) -> p k c", k=16)
        pw16 = sb1("pw16", [128, 4, 128], BF16)

        for kc in range(8):
            p.dma("pool", w_in_sb[:, kc, SSDW:INC], w_in[kc * 128:(kc + 1) * 128, SSDW:INC],
                  writes=[f"w_in_b{kc}"])
        for g in range(4):
            p.dma("pool", pw16[:, g, :], pool_w[g], writes=["pw16"])
        for kc in range(8):
            p.dma("pool", w_in_sb[:, kc, 0:SSDW], w_in[kc * 128:(kc + 1) * 128, 0:SSDW],
                  writes=[f"w_in_a{kc}"])
        for kc in range(16):
            p.dma("pool", w_out_sb[:, kc, :], w_out[kc * 128:(kc + 1) * 128, :], writes=[f"w_out{kc}"])
        WINA = [f"w_in_a{kc}" for kc in range(8)]
        WINB = [f"w_in_b{kc}" for kc in range(8)]
        WOUT = [f"w_out{kc}" for kc in range(16)]

        junk = sb1("junk", [128, D], BF16)
        xin = [sb1(f"xin{i}", [128, D]) for i in range(2)]
        hnT = [sb1(f"hnT{i}", [128, 8, T1], BF16) for i in range(2)]
        ssq = sb1("ssq", [128, 8])
        ssq4 = sb1("ssq4", [128, 4])

        def norm_front(nrows, xin_t, xin_key, sc):
            sq = ssq[0:nrows, sc:sc + 1]
            p.op("act", ACT(junk[0:nrows, :], xin_t[0:nrows, :], AF.Square, accum_out=sq),
                 reads=[xin_key], writes=["junk", f"ssq{sc}"])
            rstd_from_ssq(sq, D, nrows, f"ssq{sc}")
            p.op("act", ACT(xin_t[0:nrows, :], xin_t[0:nrows, :], AF.Copy, scale=sq),
                 reads=[xin_key, f"ssq{sc}"], writes=[xin_key])

        def norm_back(nrows, xin_t, xin_key, dstT, col0, nw, wkeys, banks=(0, 1)):
            for half in range(2):
                b = banks[half]
                p.group("pe", [TR(ps[:, b, i * nrows:(i + 1) * nrows],
                                  xin_t[0:nrows, (half * 4 + i) * 128:(half * 4 + i + 1) * 128],
                                  ident[0:nrows, 0:nrows]) for i in range(4)],
                        reads=[xin_key, "ident"], writes=pk(b))
                p.op("dve", TT(dstT[:, half * 4:half * 4 + 4, col0:col0 + nrows],
                               ps[:, b, 0:4 * nrows].rearrange("p (k t) -> p k t", k=4),
                               bc(nw[:, half * 4:half * 4 + 4].unsqueeze(2), [128, 4, nrows]), ALU.mult),
                     reads=pk(b) + ["vecA"], writes=wkeys)

        def norm_transpose(src_rows, nrows, dstT, col0, nw, xin_t, xin_key, wkeys):
            norm_front(nrows, xin_t, xin_key, 2)
            norm_back(nrows, xin_t, xin_key, dstT, col0, nw, wkeys)

        def softplus_dt(dst, src_ps, nrows, rk, wk):
            p.op("dve", TT(dst, src_ps, dtb_bc[0:nrows, :], ALU.add), reads=rk + ["dtb_bc"], writes=[wk])
            p.op("act", ACT(dst, dst, AF.Exp), reads=[wk], writes=[wk])
            p.op("act", ACT(dst, dst, AF.Ln, bias=1.0, scale=1.0), reads=[wk], writes=[wk])

        def stageA_dma(ti):
            for j in range(2):
                p.dma("sp", xin[j][:], x_p[ti * T1 + j * 128: ti * T1 + (j + 1) * 128, :], writes=[f"xin{j}"])

        def stageA_norm(ti):
            for j in range(2):
                norm_front(128, xin[j], f"xin{j}", j)

        def stageA_tr(ti, banks):
            for j in range(2):
                norm_back(128, xin[j], f"xin{j}", hnT[ti % 2], j * 128, n1w, [f"hnT{ti % 2}_{j}"], banks=banks)

        def stage_BCD(ti, cs):
            hT = hnT[ti % 2]
            HN = [f"hnT{ti % 2}_0", f"hnT{ti % 2}_1"]
            for j in range(2):
                p.group("pe", [MM(ps[:, 2, j * 128:j * 128 + H], hT[:, kc, j * 128:(j + 1) * 128],
                                  w_in_sb[:, kc, 3584:3608], start=(kc == 0), stop=(kc == 7))
                               for kc in range(8)], reads=[HN[j]] + WINB, writes=pk(2))
            for j in range(2):
                softplus_dt(dt_tok[:, j, :], ps[:, 2, j * 128:j * 128 + H], 128, pk(2), f"dt_tok{j}")
                p.op("act", ACT(lndt[:, j, :], dt_tok[:, j, :], AF.Ln), reads=[f"dt_tok{j}"], writes=[f"lndt{j}"])
            stage_D_front(ti, hT, HN)
            Z_ops.clear()
            for j in range(2):
                for g3 in range(3):
                    def zg(j=j, g3=g3):
                        p.group("pe", [MM(ps[:, g3, :], hT[:, kc, j * 128:(j + 1) * 128], w_in_sb[:, kc, g3 * 512:(g3 + 1) * 512],
                                          start=(kc == 0), stop=(kc == 7)) for kc in range(8)],
                                reads=[HN[j]] + WINA, writes=pk(g3))
                    Z_ops.append(zg)

                def zev(j=j):
                    p.op("act", ACT(zsil[j][:], ps[:, 0:3, :].rearrange("p a b -> p (a b)"), AF.Silu),
                         reads=pk(0) + pk(1) + pk(2), writes=[f"zsil{j}"])
                Z_ops.append(zev)
            dback = stage_D_back_ops(ti)
            step = 0
            for _ in stage_C_gen(ti, hT, HN, cs, (3, 4, 7, 6), (5,)):
                if Z_ops:
                    Z_ops.pop(0)()
                step += 1
                if step > 10 and not D_pool:
                    for _k in range(2):
                        if dback:
                            dback.pop(0)()
            while Z_ops:
                Z_ops.pop(0)()
            while D_pool:
                D_pool.pop(0)()
            while dback:
                dback.pop(0)()

        def stage_C_gen(ti, hnT, HN, cs, CB3, TB):
            pend_mid = []
            pend_silu = []
            pend_tr = []
            xtk = x_tok[ti % 2]
            xtkey = f"x_tok{ti % 2}"
            for idx, c in enumerate(cs):
                b = CB3[idx % len(CB3)]
                R = Rst[c % 2]
                rk = f"Rst{c % 2}"
                accp = ps[:, b, 0:T1]
                col = SSDW + c * 128
                p.group("pe", [MM(accp, w_in_sb[:, kc, col:col + 128], hnT[:, kc, :],
                                  start=(kc == 0), stop=(kc == 7)) for kc in range(8)],
                        reads=HN + WINB, writes=pk(b))
                p.op("pool", CP(R[:, 0:3], halo16[:, c, :]), reads=[f"halo{c}"], writes=[rk + "h"])
                p.op("act", ACT(R[:, 3:3 + T1], accp, AF.Copy), reads=pk(b), writes=[rk])
                if ti == NT1_ - 1:
                    p.op("act", ACT(halo[:, c, :], accp[:, T1 - 3:T1], AF.Copy), reads=pk(b), writes=[f"halo32_{c}"])

                def mid(c=c, b=b, accp=accp, R=R, rk=rk):
                    p.op("pe", lambda e: e.matmul(accp, cdiag[:, c, :], R[:, 2:2 + T1], start=False, stop=True,
                                                  skip_group_check=True),
                         reads=[rk, rk + "h", "cdiag"] + pk(b), writes=pk(b))
                    for k in range(2):
                        p.op("dve", STT(accp, R[:, k:k + T1], crat[:, k, c:c + 1], accp, ALU.mult, ALU.add),
                             reads=[rk, rk + "h", "crat"] + pk(b), writes=pk(b))
                    p.op("pool", CP(halo16[:, c, :], R[:, T1:T1 + 3]), reads=[rk], writes=[f"halo{c}"])
                pend_mid.append(mid)
                if len(pend_mid) > 1:
                    pend_mid.pop(0)()
                if D_pool:
                    D_pool.pop(0)()

                def do_silu(c=c, b=b, accp=accp):
                    if c < 14:
                        X = xsT[c % 3]
                        xk = f"xsT{c % 3}"
                        p.op("act", ACT(X[:], accp, AF.Silu, scale=w3p[:, c:c + 1], bias=cb(c)),
                             reads=pk(b) + ["crat", "vecA"], writes=[xk])
                        if c >= 12:
                            p.op("pool", CP(BT16[:, c - 12, :], X[:]), reads=[xk], writes=["BT16"])

                        def post(c=c, X=X, xk=xk):
                            tb = TB[c % len(TB)]
                            p.group("pe", [TR(ps[:, tb, jx * 128:(jx + 1) * 128], X[:, jx * 128:(jx + 1) * 128], ident[:])
                                           for jx in range(2)], reads=[xk, "ident"], writes=pk(tb))
                            src = ps[:, tb, 0:T1].rearrange("p (j t) -> p j t", j=2)
                            if c < 12:
                                if c % 2 == 0:
                                    p.op("act", ACT(xtk[:, :, c * 128:(c + 1) * 128], src, AF.Copy),
                                         reads=pk(tb), writes=[xtkey])
                                else:
                                    p.op("dve", CP(xtk[:, :, c * 128:(c + 1) * 128], src),
                                         reads=pk(tb), writes=[xtkey])
                            else:
                                for a in range(2):
                                    g = 2 * (c - 12) + a
                                    p.op("dve", CP(Bz[:, :, g, a * 64:(a + 1) * 64], src[:, :, a * 64:(a + 1) * 64]),
                                         reads=pk(tb), writes=["Bz"])
                        pend_tr.append(post)
                    else:
                        for a in range(2):
                            g = 2 * (c - 14) + a
                            p.op("act", ACT(Cz[a * 64:(a + 1) * 64, g, :], accp[a * 64:(a + 1) * 64, :], AF.Silu,
                                            scale=w3p[a * 64:(a + 1) * 64, c:c + 1], bias=cb(c)[a * 64:(a + 1) * 64, :]),
                                 reads=pk(b) + ["crat", "vecA"], writes=["Cz"])
                pend_silu.append(do_silu)
                if len(pend_silu) > 2:
                    pend_silu.pop(0)()
                if len(pend_tr) > 1:
                    pend_tr.pop(0)()
                yield
            while pend_mid:
                pend_mid.pop(0)()
            while pend_silu:
                pend_silu.pop(0)()
            while pend_tr:
                pend_tr.pop(0)()
            yield

        def stage_D_front(ti, hnT, HN):
            for c in range(4):
                b = 5 + (c % 2)
                col = 3608 + c * 128
                p.group("pe", [MM(ps[:, b, 0:T1], w_in_sb[:, kc, col:col + 128], hnT[:, kc, :],
                                  start=(kc == 0), stop=(kc == 7)) for kc in range(8)],
                        reads=HN + WINB, writes=pk(b))
                if ti > 0:
                    p.op("pool", CP(V[:, c, 0:15], V[:, c, T1:T1 + 15]), reads=[f"V{c}"], writes=[f"V{c}"])
                p.op("act", ACT(V[:, c, 15:15 + T1], ps[:, b, 0:T1], AF.Copy), reads=pk(b), writes=[f"V{c}"])
            L = 15 + T1
            D_fin.clear()
            D_pool.clear()
            for g in range(4):
                w = 2 << g
                cur = V[:, g, :]
                ck = f"V{g}"
                nsteps = g + 1
                step = 1
                lo = 0
                for si in range(nsteps):
                    if si == nsteps - 1:
                        nb, nk = SRES[g], f"SRES{g}"
                    else:
                        nb, nk = SSCR[si % 2], f"SSCR{si % 2}"
                    lo2 = lo + step

                    def sum_op(nb=nb, nk=nk, cur=cur, ck=ck, lo2=lo2, step=step):
                        p.op("pool", TT(nb[:, lo2:L], cur[:, lo2:L], cur[:, lo2 - step:L - step], ALU.add),
                             reads=[ck], writes=[nk])
                    D_pool.append(sum_op)
                    cur, ck, lo = nb, nk, lo2
                    step *= 2
                D_fin.append((g, w, cur, ck))

        def stage_D_back_ops(ti):
            L = 15 + T1
            ops = []
            for (g, w, cur, ck) in D_fin:
                def stt(g=g, w=w, cur=cur, ck=ck):
                    p.op("dve", STT(mT16[:, g, :], cur[:, 15:L], 1.0 / w, V[:, g, 15:L], ALU.mult, ALU.subtract),
                         reads=[ck, f"V{g}"], writes=[f"mT16_{g}"])
                    if ti == 0:
                        n = w - 1
                        p.op("dve", TT(ybuf[:, 0:n], cur[:, 15:15 + n], invc[:, 0:n], ALU.mult),
                             reads=[ck, "invc"], writes=["ybuf"])
                        p.op("dve", TT(mT16[:, g, 0:n], ybuf[:, 0:n], V[:, g, 15:15 + n], ALU.subtract),
                             reads=["ybuf", f"V{g}"], writes=[f"mT16_{g}"])
                ops.append(stt)
            for g in range(4):
                def mm(g=g):
                    b = g // 2
                    p.op("pe", MM(ps[:, b, (g % 2) * T1:(g % 2 + 1) * T1], pw16[:, g, :], mT16[:, g, :]),
                         reads=["pw16", f"mT16_{g}"], writes=pk(b))
                ops.append(mm)
            for g in range(4):
                def ev(g=g):
                    b = g // 2
                    p.op("act", ACT(ycatT[:, 12 + g, :], ps[:, b, (g % 2) * T1:(g % 2 + 1) * T1], AF.Copy,
                                    scale=pscale[:, g:g + 1]), reads=pk(b) + ["vecA"], writes=["ycatT_p"])
                ops.append(ev)
            return ops

        def chunk_prologue(ti, j):
            tsl = slice(j * 128, (j + 1) * 128)
            dtA, acum, nacum, tmp, eacum, cdec, w2 = (sm[:, i, :] for i in range(7))
            dtj = dt_tok[:, j, :]
            p.op("dve", TT(dtA, dtj, a_bc[:], ALU.mult), reads=[f"dt_tok{j}", "a_bc"], writes=["dtA"])
            p.op("dve", CP(dtA16[:, 0, :], dtA), reads=["dtA"], writes=["dtA16"])
            p.op("dve", TT(dtA16[:, 1, :], dtA, dtA16[:, 0, :], ALU.subtract), reads=["dtA", "dtA16"], writes=["dtA16"])
            p.group("pe", [MM(ps[:, 6, 0:H], tri[:], dtA), MM(ps[:, 6, 32:32 + H], ones[:], dtA)],
                    reads=["tri", "ones", "dtA"], writes=pk(6))
            p.group("pe", [MM(ps[:, 7, g * 128:(g + 1) * 128], BT16[:, g // 2, tsl], Cz[:, g, tsl])
                           for g in range(4)], reads=["BT16", "Cz"], writes=pk(7))
            p.op("dve", CP(acum, ps[:, 6, 0:H]), reads=pk(6), writes=["acum"])
            p.op("dve", TT(tmp, ps[:, 6, 32:32 + H], acum, ALU.subtract), reads=pk(6) + ["acum"], writes=["dte"])
            p.op("act", ACT(cdec, ps[:, 6, 32:32 + H], AF.Exp), reads=pk(6), writes=["cdec"])
            p.op("dve", TT(nacum, lndt[:, j, :], acum, ALU.subtract), reads=["acum", f"lndt{j}"], writes=["nacum"])
            p.op("act", ACT(CBsb[:].rearrange("p g l -> p (g l)"), ps[:, 7, :], AF.Copy), reads=pk(7), writes=["CBsb"])
            p.op("act", ACT(tmp, tmp, AF.Exp), reads=["dte"], writes=["dte"])
            p.op("act", ACT(eacum, acum, AF.Exp), reads=["acum"], writes=["eacum"])
            p.op("dve", TT(w2, tmp, dtj, ALU.mult), reads=["dte", f"dt_tok{j}"], writes=["w2"])
            for a in range(2):
                p.op("dve", CP(cdecH[a * 64:(a + 1) * 64, :].rearrange("p (jj r) -> p jj r", jj=2),
                               cdec[a * 64:(a + 1) * 64, :].rearrange("p (jj x) -> p jj x", jj=2)[:, :, 6 * a:6 * a + 6]),
                     reads=["cdec"], writes=["cdecH"])

        def ssd_chunk(ti, j, pre_heads=None, gap=None, defer_out=False, hidden=None, skip_prologue=False, early_next=None):
            hT = hnT[ti % 2]
            xtk = x_tok[ti % 2]
            xtkey = f"x_tok{ti % 2}"
            tok0 = ti * T1 + j * 128
            first = (ti == 0 and j == 0)
            last = (ti == SEQ // T1 - 1 and j == 1)
            tsl = slice(j * 128, (j + 1) * 128)
            dtA, acum, nacum, tmp, eacum, cdec, w2 = (sm[:, i, :] for i in range(7))
            dtj = dt_tok[:, j, :]
            if not skip_prologue:
                chunk_prologue(ti, j)
            xt3 = xtk[:, j, :].rearrange("p (h d) -> p h d", h=H)

            def b3(v):
                return bc(v.unsqueeze(2), [128, H, P])
            mms = []
            for g in range(4):
                c0 = g * 384
                rbase = (g // 2) * 384
                off = 0
                while off < 384:
                    col = c0 + off
                    n = min(384 - off, 512 - (col % 512))
                    mms.append(MM(ps[:, 3 + col // 512, (col % 512):(col % 512) + n], Cz[:, g, tsl],
                                  H16[:, rbase + off:rbase + off + n]))
                    off += n
            p.group("pe", mms, reads=["Cz", "H16"], writes=pk(3) + pk(4) + pk(5))
            p.op("dve", TT(y1b[:].rearrange("p (h d) -> p h d", h=H),
                           ps[:, 3:6, :].rearrange("p a (h d) -> p (a h) d", d=P), b3(eacum), ALU.mult),
                 reads=pk(3) + pk(4) + pk(5) + ["eacum"], writes=["y1b"])
            p.op("pool", TT(xw[:].rearrange("p (h d) -> p h d", h=H), xt3, b3(w2), ALU.mult),
                 reads=[xtkey, "w2"], writes=["xw"])
            p.op("pool", TT(H32[:].rearrange("p (h d) -> p h d", d=P), H32[:].rearrange("p (h d) -> p h d", d=P),
                            bc(cdecH[:].unsqueeze(2), [128, 12, P]), ALU.mult),
                 reads=["H32", "cdecH"], writes=["H32"])
            NB = 8

            def batch_front(k):
                ab = 6 + (k % 2)
                hs = range(4 * k, 4 * k + 4)
                mms = []
                for h in hs:
                    A_ps = ps[:, ab, (h % 4) * 128:(h % 4 + 1) * 128]
                    mms.append(MM(A_ps, bc(dtA16[:, 0, h:h + 1], [128, 128]), tri16[:], start=True, stop=False))
                    mms.append(MM(A_ps, bc(dtA16[:, 1, h:h + 1], [128, 128]), tri16[:], start=False, stop=False))
                    mms.append(MM(A_ps, identb[:], negtri[:], start=False, stop=True))
                p.group("pe", mms, reads=["dtA16", "tri16", "identb", "negtri"], writes=pk(ab))
                for h in hs:
                    A_ps = ps[:, ab, (h % 4) * 128:(h % 4 + 1) * 128]
                    p.op("act", ACT(Eh[h % NB][:], A_ps, AF.Exp, bias=nacum[:, h:h + 1], scale=1.0),
                         reads=pk(ab) + ["nacum"], writes=[f"Eh{h % NB}"])
                for h in hs:
                    p.op("dve", TT(Mh[h % NB][:], Eh[h % NB][:], CBsb[:, h // 6, :], ALU.mult),
                         reads=[f"Eh{h % NB}", "CBsb"], writes=[f"Mh{h % NB}"])

            def batch_back(k):
                for h in range(4 * k, 4 * k + 4):
                    col = h * 64
                    yo = ps[:, col // 512, (col % 512):(col % 512) + 64]
                    lastb = (h % 8 == 7)
                    p.group("pe", [lambda e, yo=yo, h=h, col=col: e.matmul(yo, Mh[h % NB][:], xtk[:, j, col:col + 64],
                                                                              start=False, stop=False, skip_group_check=True),
                                   lambda e, yo=yo, h=h, col=col, lastb=lastb: e.matmul(yo, Dmat[:, h, :], xtk[:, j, col:col + 64],
                                                                                         start=False, stop=lastb, skip_group_check=True)],
                            reads=[f"Mh{h % NB}", xtkey, "Dmat"], writes=pk(col // 512))
            batch_front(0)
            batch_front(1)
            p.group("pe", [lambda e, b=b: e.matmul(ps[:, b, :], identb[:], y1b[:, b * 512:(b + 1) * 512], start=True, stop=False,
                                                    skip_group_check=True) for b in range(3)],
                    reads=["identb", "y1b"], writes=pk(0) + pk(1) + pk(2))
            def hid(n):
                if hidden is not None:
                    for _ in range(n):
                        next(hidden, None)
            def state_update():
                for jj in range(2):
                    b = 3 + jj
                    p.group("pe", [MM(ps[:, b, 0:384], Bz[:, j, 2 * jj + a, :], xw[:, (2 * jj + a) * 384:(2 * jj + a + 1) * 384],
                                      start=(a == 0), stop=(a == 1)) for a in range(2)],
                            reads=["Bz", "xw"], writes=pk(b))
                for jj in range(2):
                    b = 3 + jj
                    p.op("dve", TT(H32[:, jj * 384:(jj + 1) * 384], ps[:, b, 0:384], H32[:, jj * 384:(jj + 1) * 384],
                                   ALU.add), reads=pk(b) + ["H32"], writes=["H32"])
                if not last:
                    p.op("act", ACT(H16[:], H32[:], AF.Copy), reads=["H32"], writes=["H16"])
            batch_back(0)
            state_update()
            hid(2)
            for k in range(2, 6):
                batch_front(k)
                batch_back(k - 1)
                hid(2)
            batch_back(5)
            hid(3)
            yd = ps[:, 0:3, :].rearrange("p a b -> p (a b)")
            p.op("dve", TT(ybuf[:], yd, zsil[j][:], ALU.mult), reads=pk(0) + pk(1) + pk(2) + [f"zsil{j}"], writes=["ybuf"])
            YK = ["ybuf"]
            for g in range(4):
                if g < 2:
                    p.op("act", ACT(junk[:, 0:384], ybuf[:, g * 384:(g + 1) * 384], AF.Square,
                                    accum_out=ssq4[:, g:g + 1]), reads=YK, writes=["junk", f"ssq4_{g}"])
                else:
                    p.op("dve", lambda e, g=g: e.scalar_tensor_tensor(junkB[:], ybuf[:, g * 384:(g + 1) * 384], 1.0,
                                                                     ybuf[:, g * 384:(g + 1) * 384], ALU.mult, ALU.mult,
                                                                     accum_out=ssq4[:, g:g + 1]),
                         reads=YK, writes=["junkB", f"ssq4_{g}"])
            SK = [f"ssq4_{g}" for g in range(4)]
            p.op("act", ACT(ssq4[:], ssq4[:], AF.Ln, scale=1.0 / 384, bias=eps_t[:]), reads=SK + ["eps"], writes=SK)
            p.op("act", ACT(ssq4[:], ssq4[:], AF.Exp, scale=-0.5), reads=SK, writes=SK)
            for g in range(4):
                if g % 2 == 0:
                    p.op("act", ACT(ybuf[:, g * 384:(g + 1) * 384], ybuf[:, g * 384:(g + 1) * 384], AF.Copy,
                                    scale=ssq4[:, g:g + 1]), reads=YK + SK, writes=[f"ybufN{g}"])
                else:
                    p.op("dve", TS(ybuf[:, g * 384:(g + 1) * 384], ybuf[:, g * 384:(g + 1) * 384], ssq4[:, g:g + 1], None,
                                   ALU.mult), reads=YK + SK, writes=[f"ybufN{g}"])
            if gap is not None:
                gap()
            YN = [f"ybufN{g}" for g in range(4)]
            for b3i in range(3):
                b = 3 + b3i
                p.group("pe", [TR(ps[:, b, i * 128:(i + 1) * 128], ybuf[:, (b3i * 4 + i) * 128:(b3i * 4 + i + 1) * 128],
                                  ident[:]) for i in range(4)], reads=YN + YK + ["ident"], writes=pk(b))
                if b3i != 1:
                    p.op("dve", TT(ycatT[:, b3i * 4:b3i * 4 + 4, tsl],
                                   ps[:, b, :].rearrange("p (k t) -> p k t", k=4),
                                   bc(ssdw[:, b3i * 4:b3i * 4 + 4].unsqueeze(2), [128, 4, 128]), ALU.mult),
                         reads=pk(b) + ["vecA"], writes=[f"ycatT_s{j}_{b3i}"])
                else:
                    p.group("act", [ACT(ycatT[:, 4 + i, tsl], ps[:, b, i * 128:(i + 1) * 128], AF.Copy,
                                        scale=ssdw[:, 4 + i:5 + i]) for i in range(4)],
                            reads=pk(b) + ["vecA"], writes=[f"ycatT_s{j}_{b3i}"])
            if early_next is not None:
                early_next()
            if pre_heads is not None:
                pre_heads()
            if not defer_out:
                do_out(ti, j, (0, 1))

        def do_out(ti, j, banks):
            tok0 = ti * T1 + j * 128
            xr = xres[j]
            p.dma("sp", xr[:], x_p[tok0:tok0 + 128, :], writes=[f"xres{j}"])
            out_proj(ycatT, slice(j * 128, (j + 1) * 128), 128, [f"ycatT_s{j}_{i}" for i in range(3)] + ["ycatT_p"],
                     xr, f"xres{j}", tok0, banks=banks)

        def out_proj(yT, tsl, nrows, ykeys, xr, xrk, row0, banks=(0, 1)):
            for hh in range(2):
                b = banks[hh]
                p.group("pe", [MM(ps[0:nrows, b, :], yT[:, kc, tsl], w_out_sb[:, kc, hh * 512:(hh + 1) * 512],
                                  start=(kc == 0), stop=(kc == 15)) for kc in range(16)],
                        reads=ykeys + WOUT, writes=pk(b))
                p.op("dve", TT(xr[0:nrows, hh * 512:(hh + 1) * 512], ps[0:nrows, b, :],
                               xr[0:nrows, hh * 512:(hh + 1) * 512], ALU.add),
                     reads=pk(b) + [xrk], writes=[xrk])
            p.dma("sp", x2s[row0:row0 + nrows, :], xr[0:nrows, :], reads=[xrk], writes=[f"x2s{row0}"])

        def sample_mixer():
            with contextlib.ExitStack() as sts:
                def sbs(name, shape, dt=F32):
                    return sts.enter_context(nc.sbuf_tensor(name, list(shape), dt))
                xs_in = sbs("xs_in", [NS, D])
                hnTs = sbs("hnTs", [128, 8, NS], BF16)
                dt_s = sbs("dt_s", [NS, H])
                zs = sbs("zs", [NS, SSDW])
                Rs = sbs("Rs", [128, 16, NS])
                stc_tok = sbs("stc_tok", [48, CONVD])
                stcT = sbs("stcT", [128, 16, NS, 3])
                accs = sbs("accs", [128, 16, NS])
                tmps = sbs("tmps", [128, 16, NS])
                xa_s = sbs("xa_s", [128, 16, NS])
                xbc_tok = sbs("xbc_tok", [NS, CONVD])
                vTs = sbs("vTs", [128, 4, NS])
                v_tok = sbs("v_tok", [NS, POOLW])
                stp_tok = [sbs(f"stp_tok{i}", [120, POOLW]) for i in range(2)]
                stpT = sbs("stpT", [128, 4, NS, 15])
                wsum = sbs("wsum", [128, NS])
                mTs = sbs("mTs", [128, 4, NS], BF16)
                ycatTs = sbs("ycatTs", [128, 16, NS], BF16)
                sms = sbs("sms", [NS, 4, H])
                dtx_s = sbs("dtx_s", [NS, SSDW])
                dtx_r = sbs("dtx_r", [128, 3, P])
                dec_r = sbs("dec_r", [128, 3])
                B_r = sbs("B_r", [128, NST])
                C_r = sbs("C_r", [128, NST])
                Hb = [sbs(f"Hb{i}", [128, 16, NST]) for i in range(3)]
                Tb2 = [sbs(f"Tb{i}", [128, 16, NST]) for i in range(2)]
                Tb = Tb2[0]
                Pb = sbs("Pb", [128, 16, NST])
                cb_r = sbs("cb_r", [128, 1])
                y_r = sbs("y_r", [128, 3, P])
                y_tok = sbs("y_tok", [NS, SSDW])

                p.dma("sp", xs_in[:], x_s, writes=["xs_in"])
                p.dma("sp", stc_tok[:], st_conv.rearrange("b k c -> (b k) c"), writes=["stc_tok"])
                stp_flat = st_pool.rearrange("b j c -> (b j) c")
                for i in range(2):
                    p.dma("sp", stp_tok[i][:], stp_flat[i * 120:(i + 1) * 120, :], writes=[f"stp_tok{i}"])
                norm_transpose(None, NS, hnTs, 0, n1w, xs_in, "xs_in", ["hnTs"])
                p.group("pe", [MM(ps[0:NS, 2, 0:H], hnTs[:, kc, :], w_in_sb[:, kc, 3584:3608],
                                  start=(kc == 0), stop=(kc == 7)) for kc in range(8)],
                        reads=["hnTs"] + WINB, writes=pk(2))
                softplus_dt(dt_s[:], ps[0:NS, 2, 0:H], NS, pk(2), "dt_s")
                for c in range(16):
                    col = SSDW + c * 128
                    p.group("pe", [MM(ps[:, 6, c * NS:(c + 1) * NS], w_in_sb[:, kc, col:col + 128], hnTs[:, kc, :],
                                      start=(kc == 0), stop=(kc == 7)) for kc in range(8)],
                            reads=["hnTs"] + WINB, writes=pk(6))
                p.op("act", ACT(Rs[:].rearrange("p c b -> p (c b)"), ps[:, 6, 0:16 * NS], AF.Copy),
                     reads=pk(6), writes=["Rs"])
                for half in range(2):
                    b = 0 + half
                    p.group("pe", [TR(ps[:, b, i * 48:(i + 1) * 48], stc_tok[0:48, (half * 8 + i) * 128:(half * 8 + i + 1) * 128],
                                      ident[0:48, 0:48]) for i in range(8)],
                            reads=["stc_tok", "ident"], writes=pk(b))
                    p.op("dve", CP(stcT[:, half * 8:half * 8 + 8, :, :].rearrange("p c b k -> p c (b k)"),
                                   ps[:, b, 0:384].rearrange("p (c x) -> p c x", c=8)),
                         reads=pk(b), writes=["stcT"])
                cw3 = vecA[:, 16:80].rearrange("p (k c) -> p k c", k=4)

                def bcw(k):
                    return bc(cw3[:, k, :].unsqueeze(2), [128, 16, NS])
                p.op("dve", TT(accs[:], Rs[:], bcw(3), ALU.mult), reads=["Rs", "vecA"], writes=["accs"])
                p.op("dve", TT(accs[:], accs[:], bc(vecA[:, 80:96].unsqueeze(2), [128, 16, NS]), ALU.add),
                     reads=["accs", "vecA"], writes=["accs"])
                for k in range(3):
                    p.op("dve", TT(tmps[:], stcT[:, :, :, k], bcw(k), ALU.mult), reads=["stcT", "vecA"], writes=["tmps"])
                    p.op("dve", TT(accs[:], accs[:], tmps[:], ALU.add), reads=["accs", "tmps"], writes=["accs"])
                p.op("act", ACT(xa_s[:], accs[:], AF.Silu), reads=["accs"], writes=["xa_s"])
                for pi, (src, sk) in enumerate(((Rs, "Rs"), (xa_s, "xa_s"))):
                    for q4 in range(4):
                        b = 0 + (q4 % 2)
                        p.group("pe", [TR(ps[0:NS, b, i * 128:(i + 1) * 128], src[:, q4 * 4 + i, :], ident[:])
                                       for i in range(4)], reads=[sk, "ident"], writes=pk(b))
                        p.op("act", ACT(xbc_tok[:, q4 * 512:(q4 + 1) * 512], ps[0:NS, b, :], AF.Copy),
                             reads=pk(b), writes=["xbc_tok"])
                    if pi == 0:
                        p.dma("sp", conv_s[:, 2, :], xbc_tok[:], reads=["xbc_tok"], store=True)
                p.dma("act", conv_s[:, 0:2, :], st_conv[:, 1:3, :], store=True)
                for c in range(4):
                    col = 3608 + c * 128
                    p.group("pe", [MM(ps[:, 7, c * NS:(c + 1) * NS], w_in_sb[:, kc, col:col + 128], hnTs[:, kc, :],
                                      start=(kc == 0), stop=(kc == 7)) for kc in range(8)],
                            reads=["hnTs"] + WINB, writes=pk(7))
                p.op("act", ACT(vTs[:].rearrange("p c b -> p (c b)"), ps[:, 7, 0:4 * NS], AF.Copy),
                     reads=pk(7), writes=["vTs"])
                p.group("pe", [TR(ps[0:NS, 2, i * 128:(i + 1) * 128], vTs[:, i, :], ident[:]) for i in range(4)],
                        reads=["vTs", "ident"], writes=pk(2))
                p.op("act", ACT(v_tok[:], ps[0:NS, 2, :], AF.Copy), reads=pk(2), writes=["v_tok"])
                p.dma("sp", pool_s[:, 14, :], v_tok[:], reads=["v_tok"], store=True)
                p.dma("act", pool_s[:, 0:14, :], st_pool[:, 1:15, :], store=True)
                for c in range(4):
                    b = 0 + (c % 2)
                    p.group("pe", [TR(ps[:, b, i * 120:(i + 1) * 120], stp_tok[i][0:120, c * 128:(c + 1) * 128],
                                      ident[0:120, 0:120]) for i in range(2)],
                            reads=["stp_tok0", "stp_tok1", "ident"], writes=pk(b))
                    p.op("dve", CP(stpT[:, c, :, :].rearrange("p b j -> p (b j)"), ps[:, b, 0:240]),
                         reads=pk(b), writes=["stpT"])
                for g in range(4):
                    w = 2 << g
                    p.op("dve", lambda e, g=g, w=w: e.tensor_reduce(wsum[:], stpT[:, g, :, 15 - (w - 1):15], AX.X, ALU.add),
                         reads=["stpT"], writes=["wsum"])
                    p.op("dve", TT(wsum[:], wsum[:], vTs[:, g, :], ALU.add), reads=["wsum", "vTs"], writes=["wsum"])
                    p.op("dve", STT(mTs[:, g, :], wsum[:], 1.0 / w, vTs[:, g, :], ALU.mult, ALU.subtract),
                         reads=["wsum", "vTs"], writes=["mTs"])
                for g in range(4):
                    p.op("pe", MM(ps[:, 7, 64 + g * NS:64 + (g + 1) * NS], pw16[:, g, :], mTs[:, g, :]),
                         reads=["pw16", "mTs"], writes=pk(7))
                    p.op("act", ACT(ycatTs[:, 12 + g, :], ps[:, 7, 64 + g * NS:64 + (g + 1) * NS], AF.Copy,
                                    scale=pscale[:, g:g + 1]), reads=pk(7) + ["vecA"], writes=["ycatTs_p"])
                for g3 in range(3):
                    b = 3 + g3
                    p.group("pe", [MM(ps[0:NS, b, :], hnTs[:, kc, :], w_in_sb[:, kc, g3 * 512:(g3 + 1) * 512],
                                      start=(kc == 0), stop=(kc == 7)) for kc in range(8)],
                            reads=["hnTs"] + WINA, writes=pk(b))
                p.op("act", ACT(zs[:], ps[0:NS, 3:6, :].rearrange("p a b -> p (a b)"), AF.Silu),
                     reads=pk(3) + pk(4) + pk(5), writes=["zs"])
                dtA_s, dec_s = sms[:, 0, :], sms[:, 1, :]
                p.op("dve", TT(dtA_s, dt_s[:], a_bc[0:NS, :], ALU.mult), reads=["dt_s", "a_bc"], writes=["dtA_s"])
                p.op("act", ACT(dec_s, dtA_s, AF.Exp), reads=["dtA_s"], writes=["dec_s"])
                p.op("dve", TT(dtx_s[:].rearrange("p (h d) -> p h d", h=H),
                               xbc_tok[:, 0:SSDW].rearrange("p (h d) -> p h d", h=H),
                               bc(dt_s[:].unsqueeze(2), [NS, H, P]), ALU.mult),
                     reads=["xbc_tok", "dt_s"], writes=["dtx_s"])
                p.dma("sp", scr_dtx, dtx_s[:], reads=["dtx_s"], writes=["scr_dtx"])
                p.dma("sp", scr_dec, dec_s, reads=["dec_s"], writes=["scr_dec"])
                for rep in range(2):
                    p.dma("sp", scr_B[:, rep::2, :], xbc_tok[:, 1536:1792].rearrange("p (g n) -> p g n", g=4),
                          reads=["xbc_tok"], writes=["scr_B"])
                    p.dma("sp", scr_C[:, rep::2, :], xbc_tok[:, 1792:2048].rearrange("p (g n) -> p g n", g=4),
                          reads=["xbc_tok"], writes=["scr_C"])
                p.dma("sp", dtx_r[:].rearrange("p s d -> p (s d)"), scr_dtx.rearrange("b (j f) -> (b j) f", j=8),
                      reads=["scr_dtx"], writes=["dtx_r"])
                p.dma("sp", dec_r[:], scr_dec.rearrange("b (j s) -> (b j) s", j=8), reads=["scr_dec"], writes=["dec_r"])
                p.dma("sp", B_r[:], scr_B.rearrange("b j n -> (b j) n"), reads=["scr_B"], writes=["B_r"])
                p.dma("sp", C_r[:], scr_C.rearrange("b j n -> (b j) n"), reads=["scr_C"], writes=["C_r"])
                ssm_in = st_ssm.rearrange("b (j s) f -> (b j) s f", j=8)
                ssm_out = ssm_s.rearrange("b (j s) f -> (b j) s f", j=8)
                stageA_dma(0)
                stageA_norm(0)
                stageA_tr(0, (4, 5))
                p.op("dve", TT(Tb[:, 0, :], B_r[:], C_r[:], ALU.mult), reads=["B_r", "C_r"], writes=["Tb"])
                p.op("dve", lambda e: e.tensor_reduce(cb_r[:], Tb[:, 0, :], AX.X, ALU.add), reads=["Tb"], writes=["cb_r"])
                PC = 16
                its = [(s_, ph) for s_ in range(3) for ph in range(P // PC)]

                def hload(i):
                    s_, ph = its[i]
                    p.dma("sp", Hb[i % 3][:].rearrange("p a n -> p (a n)"),
                          ssm_in[:, s_, ph * PC * NST:(ph + 1) * PC * NST], writes=[f"Hb{i % 3}"])
                hload(0)
                hload(1)
                for i, (s, ph) in enumerate(its):
                    hb = Hb[i % 3]
                    hk = f"Hb{i % 3}"
                    psl = slice(ph * PC, (ph + 1) * PC)
                    p.op("pool", TT(Pb[:], hb[:], bc(C_r[:].unsqueeze(1), [128, PC, NST]), ALU.mult),
                         reads=[hk, "C_r"], writes=["Pb"])
                    p.op("dve", lambda e, s=s, psl=psl: e.tensor_reduce(y_r[:, s, psl], Pb[:], AX.X, ALU.add),
                         reads=["Pb"], writes=["y_r"])
                    tb, tk = Tb2[i % 2], ("Tb" if i % 2 == 0 else "Tb_1")
                    p.op("pool", TT(tb[:], bc(dtx_r[:, s, psl].unsqueeze(2), [128, PC, NST]),
                                    bc(B_r[:].unsqueeze(1), [128, PC, NST]), ALU.mult),
                         reads=["dtx_r", "B_r"], writes=[tk])
                    p.op("dve", STT(hb[:], hb[:], dec_r[:, s:s + 1], tb[:], ALU.mult, ALU.add),
                         reads=[hk, "dec_r", tk], writes=[hk])
                    if i + 2 < len(its):
                        hload(i + 2)
                    p.dma("act", ssm_out[:, s, ph * PC * NST:(ph + 1) * PC * NST], hb[:].rearrange("p a n -> p (a n)"),
                          reads=[hk], store=True)
                p.op("dve", TT(y_r[:], y_r[:], bc(dec_r[:].unsqueeze(2), [128, 3, P]), ALU.mult),
                     reads=["y_r", "dec_r"], writes=["y_r"])
                p.op("dve", STT(y_r[:].rearrange("p s d -> p (s d)"), dtx_r[:].rearrange("p s d -> p (s d)"), cb_r[:, 0:1],
                                y_r[:].rearrange("p s d -> p (s d)"), ALU.mult, ALU.add),
                     reads=["y_r", "dtx_r", "cb_r"], writes=["y_r"])
                p.dma("sp", scr_y.rearrange("b (j f) -> (b j) f", j=8), y_r[:].rearrange("p s d -> p (s d)"),
                      reads=["y_r"], writes=["scr_y"])
                p.dma("sp", y_tok[:], scr_y, reads=["scr_y"], writes=["y_tok"])
                xt3 = xbc_tok[:, 0:SSDW].rearrange("p (h d) -> p h d", h=H)
                p.op("dve", TT(dtx_s[:].rearrange("p (h d) -> p h d", h=H), xt3,
                               bc(D_bc[0:NS, :].unsqueeze(2), [NS, H, P]), ALU.mult),
                     reads=["xbc_tok", "D_bc", "scr_dtx"], writes=["dtx_s"])
                p.op("dve", TT(y_tok[:], y_tok[:], dtx_s[:], ALU.add), reads=["y_tok", "dtx_s"], writes=["y_tok"])
                p.op("dve", TT(y_tok[:], y_tok[:], zs[:], ALU.mult), reads=["y_tok", "zs"], writes=["y_tok"])
                for g in range(4):
                    p.op("act", ACT(junk[0:NS, 0:384], y_tok[:, g * 384:(g + 1) * 384], AF.Square,
                                    accum_out=ssq4[0:NS, g:g + 1]), reads=["y_tok"], writes=["junk", "ssq4"])
                p.op("act", ACT(ssq4[0:NS, :], ssq4[0:NS, :], AF.Ln, scale=1.0 / 384, bias=eps_t[0:NS, :]),
                     reads=["ssq4", "eps"], writes=["ssq4"])
                p.op("act", ACT(ssq4[0:NS, :], ssq4[0:NS, :], AF.Exp, scale=-0.5), reads=["ssq4"], writes=["ssq4"])
                for g in range(4):
                    p.op("act", ACT(y_tok[:, g * 384:(g + 1) * 384], y_tok[:, g * 384:(g + 1) * 384], AF.Copy,
                                    scale=ssq4[0:NS, g:g + 1]), reads=["y_tok", "ssq4"], writes=["y_tok"])
                p.group("pe", [TR(ps[:, 2, i * NS:(i + 1) * NS], y_tok[:, i * 128:(i + 1) * 128], ident[0:NS, 0:NS])
                               for i in range(12)], reads=["y_tok", "ident"], writes=pk(2))
                p.op("dve", TT(ycatTs[:, 0:12, :], ps[:, 2, 0:12 * NS].rearrange("p (k t) -> p k t", k=12),
                               bc(ssdw.unsqueeze(2), [128, 12, NS]), ALU.mult),
                     reads=pk(2) + ["vecA"], writes=["ycatTs_s"])
                p.dma("sp", xs_in[:], x_s, writes=["xs_in"])
                out_proj(ycatTs, slice(0, NS), NS, ["ycatTs_s", "ycatTs_p"], xs_in, "xs_in", SEQ)
                p.flush()

        sample_mixer()
        with contextlib.ExitStack() as stp:
            def sbp(name, shape, dt=F32):
                return stp.enter_context(nc.sbuf_tensor(name, list(shape), dt))
            Rst = [sbp(f"Rst{i}", [128, 4 + T1], BF16) for i in range(2)]
            halo16 = sbp("halo16", [128, 16, 3], BF16)
            cdiag = sbp("cdiag", [128, 16, 128], BF16)
            NT1_ = SEQ // T1
            xsT = [sbp(f"xsT{i}", [128, T1]) for i in range(3)]
            halo = sbp("halo", [128, 16, 3])
            crat = sbp("crat", [128, 4, 16])
            w3p = sbp("w3p", [128, 16])
            x_tok = [sbp("x_tok0", [128, 2, SSDW], BF16)] * 2
            BT16 = sbp("BT16", [128, 2, T1], BF16)
            Cz = sbp("Cz", [128, 4, T1], BF16)
            Bz = sbp("Bz", [128, 2, 4, 128], BF16)
            V = sbp("V", [128, 4, 15 + T1])
            SRES = [sbp(f"SRES{g}", [128, 15 + T1]) for g in range(4)]
            SSCR = [sbp(f"SSCR{i}", [128, 15 + T1]) for i in range(2)]
            D_fin = []
            D_pool = []
            mT16 = sbp("mT16", [128, 4, T1], BF16)
            dt_tok = sbp("dt_tok", [128, 2, H])
            lndt = sbp("lndt", [128, 2, H])
            zsil = [sbp(f"zsil{i}", [128, SSDW], BF16) for i in range(2)]
            Z_ops = []
            sm = sbp("sm", [128, 8, H])
            cdecH = sbp("cdecH", [128, 12])
            dtA16 = sbp("dtA16", [128, 2, H], BF16)
            xw = sbp("xw", [128, SSDW], BF16)
            CBsb = sbp("CBsb", [128, 4, 128], BF16)
            Eh = [sbp(f"Eh{i}", [128, 128], BF16) for i in range(8)]
            Mh = [sbp(f"Mh{i}", [128, 128], BF16) for i in range(8)]
            ybuf = sbp("ybuf", [128, SSDW])
            y1b = sbp("y1b", [128, SSDW], BF16)
            junkB = sbp("junkB", [128, 384], BF16)
            Dmat = sbp("Dmat", [128, H, 128], BF16)
            H32 = sbp("H32", [128, 768])
            H16 = sbp("H16", [128, 768], BF16)
            ycatT = sbp("ycatT", [128, 16, T1], BF16)
            xres = [sbp(f"xres{i}", [128, D]) for i in range(2)]

            p.op("pool", MS(halo16[:], 0.0), writes=[f"halo{c}" for c in range(16)])
            w3raw = vecA[:, 64:80]
            p.op("dve", TS(crat[:, 3, :], w3raw, 0.0, 1e-12, ALU.is_equal, ALU.mult), reads=["vecA"], writes=["crat"])
            p.op("dve", TT(w3p[:], crat[:, 3, :], w3raw, ALU.add), reads=["crat", "vecA"], writes=["crat"])
            p.op("dve", lambda e: e.reciprocal(crat[:, 3, :], w3p[:]), reads=["crat"], writes=["crat"])
            for k in range(3):
                p.op("dve", TT(crat[:, k, :], vecA[:, 16 + k * 16:32 + k * 16], crat[:, 3, :], ALU.mult),
                     reads=["crat", "vecA"], writes=["crat"])
            for c in range(16):
                p.op("dve", TS(cdiag[:, c, :], ident[:], crat[:, 2, c:c + 1], None, ALU.mult),
                     reads=["ident", "crat"], writes=["cdiag"])
            p.op("pool", MS(V[:], 0.0), writes=[f"V{c}" for c in range(4)])
            p.op("pool", MS(Cz[:], 0.0), writes=["Cz"])
            p.op("pool", MS(Bz[:], 0.0), writes=["Bz"])
            p.op("pool", MS(H32[:], 0.0), writes=["H32"])
            p.op("pool", MS(H16[:], 0.0), writes=["H16"])
            for h in range(H):
                p.op("dve", TS(Dmat[:, h, :], ident[:], D_bc[:, h:h + 1], None, ALU.mult),
                     reads=["ident", "D_bc"], writes=["Dmat"])


            NT1 = SEQ // T1
            stageA_dma(1)
            stageA_norm(1)
            for ti in range(NT1):
                stage_BCD(ti, list(range(16)))

                def gap0(ti=ti):
                    if ti + 1 < NT1:
                        stageA_tr(ti + 1, (6, 7))
                    if ti + 2 < NT1:
                        stageA_dma(ti + 2)
                if ti == NT1 - 1:
                    rows_out(lambda c: halo[:, c, :], 16, 3, conv_p, xin[0], [f"halo32_{c}" for c in range(16)], skey="xin0")
                    rows_out(lambda c: V[:, c, T1:T1 + 15], 4, 15, pool_p, xin[0], [f"V{c}" for c in range(4)], skey="xin0")
                ssd_chunk(ti, 0, gap=gap0, defer_out=True, early_next=lambda ti=ti: chunk_prologue(ti, 1))
                def gap1(ti=ti):
                    do_out(ti, 0, (6, 7))
                    if ti == NT1 - 1:
                        Hout = xin[1][:, 0:768].rearrange("p (q x) -> p q x", q=6)
                        for q in range(6):
                            b = 6 + (q % 2)
                            p.op("pe", TR(ps[:, b, 0:128], H32[:, q * 128:(q + 1) * 128], ident[:]), reads=["H32", "ident"],
                                 writes=pk(b))
                            p.op("act", ACT(Hout[:, q, :], ps[:, b, 0:128], AF.Copy), reads=pk(b), writes=["xin1"])
                            jj, rp = q // 3, (q % 3) * 2
                            for a in range(2):
                                h0 = 12 * jj + 6 * a + rp
                                p.dma("sp", ssm_p[h0:h0 + 2].rearrange("h p n -> (h p) n"), Hout[:, q, a * 64:(a + 1) * 64],
                                      reads=["xin1"], store=True)
                ssd_chunk(ti, 1, skip_prologue=True, gap=gap1,
                          pre_heads=(lambda ti=ti: stageA_norm(ti + 2)) if ti + 2 < NT1 else None)
            p.flush()

    with contextlib.ExitStack() as st2:
        def sb2(name, shape, dt=F32):
            return st2.enter_context(nc.sbuf_tensor(name, list(shape), dt))

        w_up_sb = sb2("w_up_sb", [128, 8, 2 * DFF], BF16)
        w_dn_sb = sb2("w_dn_sb", [128, 22, D], BF16)
        w_up_v = w_up.rearrange("(k p) c -> p k c", p=128)
        halo2 = sb2("halo2", [128, 44, 2])
        p.op("pool", MS(halo2[:], 0.0), writes=[f"halo2_{c}" for c in range(44)])
        blocks = []
        for g4 in range(6):
            lo, hi = 4 * g4, min(22, 4 * g4 + 4)
            blocks += [(lo, hi), (22 + lo, 22 + hi)]
        for bi, (lo, hi) in enumerate(blocks):
            p.dma("pool", w_up_sb[:, :, lo * 128:hi * 128], w_up_v[:, :, lo * 128:hi * 128],
                  writes=[f"w_up{cc}" for cc in range(lo, hi)])
        for fc in range(22):
            p.dma("pool", w_dn_sb[:, fc, :], w_down[fc * 128:(fc + 1) * 128, :], writes=[f"w_dn{fc}"])
        WDN = [f"w_dn{fc}" for fc in range(22)]

        ssq2 = sb2("ssq2", [128, 8])
        xin2 = [sb2(f"xin2_{i}", [128, D]) for i in range(2)]
        junk2 = sb2("junk2", [128, D], BF16)
        hn2T = sb2("hn2T", [128, 8, T2], BF16)
        RR = sb2("RR", [128, 4, 2 + T2])
        Rg = [RR[:, i, :] for i in range(2)]
        Rv = [RR[:, 2 + i, :] for i in range(2)]
        xb = [xin2[0], xin2[1], RR[:, 0:2, :].rearrange("p a b -> p (a b)")[:, 0:D],
              RR[:, 2:4, :].rearrange("p a b -> p (a b)")[:, 0:D]]
        xbk = [["xin2_0"], ["xin2_1"], ["Rg0", "Rg0h", "Rg1", "Rg1h"], ["Rv0", "Rv0h", "Rv1", "Rv1h"]]
        ag = [sb2(f"ag{i}", [128, T2]) for i in range(2)]
        actT = sb2("actT", [128, 22, T2], BF16)
        stf_tok = [sb2("stf_tok0", [32, 4 * 128])] * 2
        stfT = sb2("stfT", [128, 44, NS, 2])
        uraw = sb2("uraw", [128, 44, NS])
        rst2 = [sb2(f"rst2_{i}", [16, 512]) for i in range(2)]

        def subtiles(ntok):
            return [(j, min(128, ntok - j * 128)) for j in range((ntok + 127) // 128)]

        def ffn_norm_front(row0, ntok):
            for j, nr in subtiles(ntok):
                xt, xk = xb[j], xbk[j]
                p.dma("sp", xt[0:nr, :], x2s[row0 + j * 128: row0 + j * 128 + nr, :],
                      reads=[f"x2s{row0 + j * 128}"], writes=xk)
                sq = ssq2[0:nr, j:j + 1]
                p.op("act", ACT(junk2[0:nr, :], xt[0:nr, :], AF.Square, accum_out=sq),
                     reads=xk, writes=["junk2", f"ssq2_{j}"])
                rstd_from_ssq(sq, D, nr, f"ssq2_{j}")
                p.op("act", ACT(xt[0:nr, :], xt[0:nr, :], AF.Copy, scale=sq),
                     reads=xk + [f"ssq2_{j}"], writes=xk)

        def ffn_norm_back(ntok, banks=(0, 1, 2, 3)):
            for j, nr in subtiles(ntok):
                xt, xk = xb[j], xbk[j]
                for half in range(2):
                    b = banks[(j % 2) * 2 + half]
                    p.group("pe", [TR(ps[:, b, i * nr:(i + 1) * nr],
                                      xt[0:nr, (half * 4 + i) * 128:(half * 4 + i + 1) * 128],
                                      ident[0:nr, 0:nr]) for i in range(4)],
                            reads=xk + ["ident"], writes=pk(b))
                    p.op("dve", TT(hn2T[:, half * 4:half * 4 + 4, j * 128:j * 128 + nr],
                                   ps[:, b, 0:4 * nr].rearrange("p (k t) -> p k t", k=4),
                                   bc(n2w[:, half * 4:half * 4 + 4].unsqueeze(2), [128, 4, nr]), ALU.mult),
                         reads=pk(b) + ["vecA"], writes=[f"hn2T{j}"])

        def ffn_up(ntok, sample=False, extra=None):
            hkeys = [f"hn2T{j}" for j, _ in subtiles(ntok)]
            pend = []
            for fc in range(22):
                i2 = fc % 2
                gb = fc % 3
                vb = 3 + fc % 3
                taps = []
                for (cc, R, rk, bank) in ((fc, Rg[i2], f"Rg{i2}", gb), (22 + fc, Rv[i2], f"Rv{i2}", vb)):
                    accp = ps[:, bank, 0:ntok]
                    p.group("pe", [MM(accp, w_up_sb[:, kc, cc * 128:(cc + 1) * 128], hn2T[:, kc, 0:ntok],
                                      start=(kc == 0), stop=(kc == 7)) for kc in range(8)],
                            reads=hkeys + [f"w_up{cc}"], writes=pk(bank))
                    if not sample:
                        p.op("pool", CP(R[:, 0:2], halo2[:, cc, :]), reads=[f"halo2_{cc}"], writes=[rk + "h"])
                        p.op("act", ACT(R[:, 2:2 + ntok], accp, AF.Copy), reads=pk(bank), writes=[rk])
                        p.op("act", ACT(accp, accp, AF.Identity, scale=fw(2, cc), bias=fb(cc)),
                             reads=pk(bank) + ["vecB"], writes=pk(bank))
                        taps.append((cc, R, rk, bank, accp))
                    else:
                        p.op("act", ACT(uraw[:, cc, :], accp, AF.Copy), reads=pk(bank), writes=["uraw"])
                        p.op("act", ACT(accp, accp, AF.Identity, scale=fw(2, cc), bias=fb(cc)),
                             reads=pk(bank) + ["vecB"], writes=pk(bank))
                        for k in range(2):
                            p.op("dve", STT(accp, stfT[:, cc, :, k], fw(k, cc), accp,
                                            ALU.mult, ALU.add), reads=["stfT", "vecB"] + pk(bank), writes=pk(bank))

                for k in range(2):
                    for (cc, R, rk, bank, accp) in taps:
                        p.op("dve", STT(accp, R[:, k:k + ntok], fw(k, cc), accp, ALU.mult, ALU.add),
                             reads=[rk, rk + "h", "vecB"] + pk(bank), writes=pk(bank))
                for (cc, R, rk, bank, accp) in taps:
                    p.op("pool", CP(halo2[:, cc, :], R[:, ntok:ntok + 2]), reads=[rk], writes=[f"halo2_{cc}"])

                def fin(fc=fc, i2=i2, gb=gb, vb=vb):
                    p.op("act", ACT(ag[i2][:, 0:ntok], ps[:, gb, 0:ntok], AF.Silu), reads=pk(gb), writes=[f"ag{i2}"])
                    p.op("dve", TT(actT[:, fc, 0:ntok], ag[i2][:, 0:ntok], ps[:, vb, 0:ntok], ALU.mult),
                         reads=[f"ag{i2}"] + pk(vb), writes=[f"actT{fc}"])
                pend.append(fin)
                if len(pend) > 1:
                    pend.pop(0)()
                if extra:
                    extra.pop(0)()
            while pend:
                pend.pop(0)()
            while extra:
                extra.pop(0)()

        def ffn_up_sample():
            hkeys = ["hn2T0"]
            for half, bank in ((0, 0), (1, 3)):
                for fc in range(22):
                    cc = half * 22 + fc
                    p.group("pe", [MM(ps[:, bank, fc * NS:(fc + 1) * NS], w_up_sb[:, kc, cc * 128:(cc + 1) * 128],
                                      hn2T[:, kc, 0:NS], start=(kc == 0), stop=(kc == 7)) for kc in range(8)],
                            reads=hkeys + [f"w_up{cc}"], writes=pk(bank))
            accs = []
            for half, bank, A, akk, Tm, tk in ((0, 0, Rg[0], "Rg0", Rg[1], "Rg1"), (1, 3, Rv[0], "Rv0", Rv[1], "Rv1")):
                src = ps[:, bank, 0:22 * NS].rearrange("p (c b) -> p c b", c=22)
                ur = uraw[:, half * 22:(half + 1) * 22, :]
                acc = A[:, 0:22 * NS].rearrange("p (c b) -> p c b", c=22)
                tmp = Tm[:, 0:22 * NS].rearrange("p (c b) -> p c b", c=22)

                def wb(k, half=half):
                    return bc(vecB[:, k * 44 + half * 22:k * 44 + half * 22 + 22].unsqueeze(2), [128, 22, NS])
                p.op("act", ACT(ur, src, AF.Copy), reads=pk(bank), writes=["uraw"])
                p.op("dve", TT(acc, src, wb(2), ALU.mult), reads=pk(bank) + ["vecB"], writes=[akk])
                p.op("dve", TT(acc, acc, wb(3), ALU.add), reads=[akk, "vecB"], writes=[akk])
                for k in range(2):
                    p.op("dve", TT(tmp, stfT[:, half * 22:(half + 1) * 22, :, k], wb(k), ALU.mult),
                         reads=["stfT", "vecB"], writes=[tk])
                    p.op("dve", TT(acc, acc, tmp, ALU.add), reads=[akk, tk], writes=[akk])
                accs.append((acc, akk))
            sg = ag[0][:, 0:22 * NS].rearrange("p (c b) -> p c b", c=22)
            p.op("act", ACT(sg, accs[0][0], AF.Silu), reads=[accs[0][1]], writes=["ag0"])
            p.op("dve", TT(actT[:, :, 0:NS], sg, accs[1][0], ALU.mult), reads=["ag0", accs[1][1]],
                 writes=[f"actT{fc}" for fc in range(22)])

        def ffn_down(row0, ntok, sample=False):
            AK = [f"actT{fc}" for fc in range(22)]
            for j, nr in subtiles(ntok):
                r0 = row0 + j * 128
                for hh in range(2):
                    p.dma("sp", ag[hh][0:nr, :], x2s[r0:r0 + nr, hh * 512:(hh + 1) * 512],
                          reads=[f"x2s{r0}"], writes=[f"ag{hh}"])
                for hh in range(2):
                    b = 6 + hh
                    p.group("pe", [MM(ps[0:nr, b, :], actT[:, fc, j * 128:j * 128 + nr], w_dn_sb[:, fc, hh * 512:(hh + 1) * 512],
                                      start=(fc == 0), stop=False) for fc in range(21)],
                            reads=AK[0:21] + WDN[0:21], writes=pk(b))
                    p.op("pe", MM(ps[0:nr, b, :], actT[:, 21, j * 128:j * 128 + nr], w_dn_sb[:, 21, hh * 512:(hh + 1) * 512],
                                  start=False, stop=True), reads=AK[21:] + WDN[21:], writes=pk(b))
                    p.op("dve", TT(ag[hh][0:nr, :], ps[0:nr, b, :], ag[hh][0:nr, :], ALU.add),
                         reads=pk(b) + [f"ag{hh}"], writes=[f"ag{hh}"])
                    p.op("act", ACT(junk2[0:nr, 0:512], ag[hh][0:nr, :], AF.Square, accum_out=ssq2[0:nr, 4 + hh:5 + hh]),
                         reads=[f"ag{hh}"], writes=["junk2", f"ssq2b{hh}"])
                sq = ssq2[0:nr, 6:7]
                p.op("dve", TT(sq, ssq2[0:nr, 4:5], ssq2[0:nr, 5:6], ALU.add), reads=["ssq2b0", "ssq2b1"], writes=["ssq2c"])
                rstd_from_ssq(sq, D, nr, "ssq2c")
                for hh in range(2):
                    p.op("dve", STT(ag[hh][0:nr, :], ag[hh][0:nr, :], sq, fnw_bc[0:nr, hh * 512:(hh + 1) * 512],
                                    ALU.mult, ALU.mult), reads=[f"ag{hh}", "ssq2c", "fnw_bc"], writes=[f"ag{hh}"])
                    dst = (y_s if sample else y_p[r0:r0 + nr, :])[:, hh * 512:(hh + 1) * 512]
                    p.dma("sp", dst, ag[hh][0:nr, :], reads=[f"ag{hh}"], store=True)

        stf_flat = st_ffn.rearrange("b k c -> (b k) c")
        stf_ops = []
        for pc in range(11):
            def s_dma(pc=pc):
                p.dma("sp", stf_tok[0][:], stf_flat[:, pc * 512:(pc + 1) * 512], writes=["stf_tok0"])

            def s_tr(pc=pc):
                b = 6 + (pc % 2)
                p.group("pe", [TR(ps[:, b, i * 32:(i + 1) * 32], stf_tok[0][0:32, i * 128:(i + 1) * 128],
                                  ident[0:32, 0:32]) for i in range(4)], reads=["stf_tok0", "ident"], writes=pk(b))
                p.op("dve", CP(stfT[:, pc * 4:pc * 4 + 4, :, :].rearrange("p c b k -> p c (b k)"),
                               ps[:, b, 0:128].rearrange("p (c x) -> p c x", c=4)), reads=pk(b), writes=["stfT"])
            stf_ops += [s_dma, s_tr]
        NT2 = SEQ // T2
        ffn_norm_front(0, T2)
        ffn_norm_back(T2)
        for t in range(NT2):
            ffn_up(T2, extra=stf_ops if t == 2 else None)
            if t + 1 == NT2:
                for k in range(2):
                    p.dma("act", ffn_p[k:k + 1, :].rearrange("o (c q) -> q (o c)", q=128), halo2[:, :, k],
                          reads=[f"halo2_{c}" for c in range(44)], store=True, allow_slow_non_contiguous=True)
            if t + 1 < NT2:
                ffn_norm_front((t + 1) * T2, T2)
            else:
                ffn_norm_front(SEQ, NS)
            ffn_down(t * T2, T2)
            ffn_norm_back(T2 if t + 1 < NT2 else NS)
        ffn_up_sample()
        p.dma("act", ffn_s[:, 0, :], st_ffn[:, 1, :], store=True)
        rows_out(lambda c: uraw[:, c, :], 44, NS, ffn_s[:, 1, :], rst2, ["uraw"], banks=(2, 4, 5), skey="rst2")
        ffn_down(SEQ, NS, sample=True)
        p.flush(final=True)
    gst.close()
    return nc


_NC_CACHE = {}


def kernel(x_prompt, x_sample, state_ssm, state_conv, state_pool, state_ffn_conv,
           norm1_w, w_in, conv_w, conv_b, dt_bias, a_log, d_skip, ssd_norm_w,
           pool_w, pool_scale, w_out, norm2_w, w_up, ffn_conv_w, ffn_conv_b, w_down,
           final_norm_w):
    f = lambda a: np.ascontiguousarray(np.asarray(a, dtype=np.float32))
    if "nc" not in _NC_CACHE:
        _NC_CACHE["nc"] = build_nc()
    nc = _NC_CACHE["nc"]
    shared = {
        "norm1_w": f(norm1_w).reshape(8, 128),
        "w_in": f(w_in)[0],
        "conv_w": f(conv_w).reshape(64, 128),
        "conv_b": f(conv_b).reshape(16, 128),
        "dt_bias": f(dt_bias).reshape(1, H),
        "a_log": f(a_log).reshape(1, H),
        "d_skip": f(d_skip).reshape(1, H),
        "ssd_norm_w": f(ssd_norm_w).reshape(12, 128),
        "pool_w": f(pool_w)[0],
        "pool_scale": f(pool_scale).reshape(4, 128),
        "w_out": f(w_out)[0],
        "norm2_w": f(norm2_w).reshape(8, 128),
        "w_up": f(w_up)[0],
        "ffn_conv_w": f(ffn_conv_w).reshape(132, 128),
        "ffn_conv_b": f(ffn_conv_b).reshape(44, 128),
        "w_down": f(w_down)[0],
        "final_norm_w": f(final_norm_w).reshape(1, D),
    }
    xp, xs = f(x_prompt), f(x_sample)
    s_ssm, s_conv, s_pool, s_ffn = f(state_ssm)[0], f(state_conv)[0], f(state_pool)[0], f(state_ffn_conv)[0]
    in_maps = []
    for c in range(NCORES):
        sl = slice(c * NS, (c + 1) * NS)
        m = dict(shared)
        m["x_p"] = xp[c]
        m["x_s"] = xs[sl, 0, :]
        m["st_ssm"] = s_ssm[sl].reshape(NS, H, P * NST)
        m["st_conv"] = s_conv[sl]
        m["st_pool"] = s_pool[sl]
        m["st_ffn"] = s_ffn[sl]
        in_maps.append(m)
    res = run_bass_kernel_spmd(nc, in_maps, core_ids=list(range(NCORES)))
    R = res.results
    cat = lambda k: np.concatenate([np.asarray(r[k]) for r in R], axis=0)
    stk = lambda k: np.stack([np.asarray(r[k]) for r in R], axis=0)
    y_prompt = stk("y_p")
    y_sample = cat("y_s").reshape(NCORES * NS, 1, D)
    ssm_prompt = stk("ssm_p")[None]
    ssm_sample = cat("ssm_s").reshape(1, NCORES * NS, H, P, NST)
    conv_prompt = stk("conv_p")[None]
    conv_sample = cat("conv_s")[None]
    pool_prompt = stk("pool_p")[None]
    pool_sample = cat("pool_s")[None]
    ffn_prompt = stk("ffn_p")[None]
    ffn_sample = cat("ffn_s")[None]
    return (y_prompt, y_sample, ssm_prompt, ssm_sample, conv_prompt, conv_sample,
            pool_prompt, pool_sample, ffn_prompt, ffn_sample)
```
